# Optimizing a Trainium2 kernel written in Bass

```python
import math
import jax, jax.numpy as jnp
from jax import lax
import numpy as np

D_MODEL = 1024
BATCH = 8
SEQ = 2048
DEPTH = 2

N_BRANCH = 4
BRANCH_W = D_MODEL // 2
DN_HEADS = 4
DN_DK = BRANCH_W // DN_HEADS
DN_DV = BRANCH_W // DN_HEADS
DN_CHUNK = 64
CONV_W = 4
GM_GROUPS = 4
GM_CH = BRANCH_W // GM_GROUPS
GM_CHUNK = 128
SW_HEADS = 8
SW_KV_HEADS = 2
SW_HD = BRANCH_W // SW_HEADS
SW_WINDOW = 128
SW_BLOCK = 128
MEM_LEN = 256
XM_HEADS = 4
XM_HD = BRANCH_W // XM_HEADS

EPS = 1e-6
NEG_INF = -1e30

IN_SPLITS = [
    3 * BRANCH_W,
    BRANCH_W,
    DN_HEADS,
    DN_HEADS,
    2 * BRANCH_W,
    BRANCH_W,
    SW_HEADS * SW_HD,
    SW_KV_HEADS * SW_HD,
    SW_KV_HEADS * SW_HD,
    BRANCH_W,
    XM_HEADS * XM_HD,
    BRANCH_W,
    N_BRANCH * D_MODEL,
]
D_IN = sum(IN_SPLITS)

kernel_name = "hybrid_parallel_gated_deltanet_gmlp_swa_memory"


def _rmsnorm(x, g):
    xf = x.astype(jnp.float32)
    y = xf * lax.rsqrt(jnp.mean(xf * xf, axis=-1, keepdims=True) + EPS)
    return (y * g.astype(jnp.float32)).astype(x.dtype)


def _l2norm(x):
    return x * lax.rsqrt(jnp.sum(x * x, axis=-1, keepdims=True) + EPS)


def _split_cols(cols):
    idx, acc = [], 0
    for s in IN_SPLITS[:-1]:
        acc += s
        idx.append(acc)
    return jnp.split(cols, idx, axis=-1)


def _causal_dwconv(x, w):
    c = x.shape[-1]
    return lax.conv_general_dilated(
        x, w[:, None, :].astype(x.dtype), window_strides=(1,),
        padding=[(CONV_W - 1, 0)], dimension_numbers=("NWC", "WIO", "NWC"),
        feature_group_count=c)


def _gated_delta_chunked(q, k, v, g, beta):
    B, T, H, dk = q.shape
    dv = v.shape[-1]
    C = DN_CHUNK
    n = T // C

    def blk(a):
        return jnp.moveaxis(a.reshape((B, n, C, H) + a.shape[3:]), 3, 1)

    q, k, v, g, beta = blk(q), blk(k), blk(v), blk(g), blk(beta)
    gc = jnp.cumsum(g, axis=-1)
    diff = gc[..., :, None] - gc[..., None, :]
    ii = jnp.arange(C)
    strict = ii[:, None] > ii[None, :]
    incl = ii[:, None] >= ii[None, :]
    kb = k * beta[..., None]
    L = jnp.where(strict, jnp.einsum("bhncd,bhnsd->bhncs", kb, k)
                  * jnp.exp(jnp.where(strict, diff, 0.0)), 0.0)
    eye = jnp.eye(C, dtype=q.dtype)
    rhs = jnp.concatenate([v * beta[..., None], kb * jnp.exp(gc)[..., None]], axis=-1)
    sol = lax.linalg.triangular_solve(eye + L, rhs, left_side=True, lower=True,
                                      unit_diagonal=True)
    u, w = sol[..., :dv], sol[..., dv:]
    a_qk = jnp.where(incl, jnp.einsum("bhncd,bhnsd->bhncs", q, k)
                     * jnp.exp(jnp.where(incl, diff, 0.0)), 0.0)
    g_last = gc[..., -1]
    qg = q * jnp.exp(gc)[..., None]
    kd = k * jnp.exp(g_last[..., None] - gc)[..., None]
    d_last = jnp.exp(g_last)
    xs = tuple(jnp.moveaxis(a, 2, 0) for a in (qg, kd, u, w, a_qk, d_last))

    def step(S, inp):
        qg_c, kd_c, u_c, w_c, a_c, d_c = inp
        v_new = u_c - jnp.einsum("bhck,bhkv->bhcv", w_c, S)
        o = (jnp.einsum("bhck,bhkv->bhcv", qg_c, S)
             + jnp.einsum("bhcs,bhsv->bhcv", a_c, v_new))
        S = S * d_c[..., None, None] + jnp.einsum("bhck,bhcv->bhkv", kd_c, v_new)
        return S, o

    S0 = jnp.zeros((B, H, dk, dv), jnp.float32)
    _, o = lax.scan(step, S0, xs)
    o = jnp.moveaxis(o, 0, 2)
    return jnp.moveaxis(o, 1, 3).reshape(B, T, H, dv)


def _deltanet_branch(qkv, z, b_logit, a_logit, conv_w, a_log, dt_bias, o_norm):
    B, T, _ = qkv.shape
    qkv = jax.nn.silu(_causal_dwconv(qkv, conv_w)).astype(jnp.float32)
    q, k, v = jnp.split(qkv, 3, axis=-1)
    q = _l2norm(q.reshape(B, T, DN_HEADS, DN_DK)) * (DN_DK ** -0.5)
    k = _l2norm(k.reshape(B, T, DN_HEADS, DN_DK))
    v = v.reshape(B, T, DN_HEADS, DN_DV)
    beta = jax.nn.sigmoid(b_logit.astype(jnp.float32))
    g = -jnp.exp(a_log.astype(jnp.float32)) * jax.nn.softplus(
        a_logit.astype(jnp.float32) + dt_bias.astype(jnp.float32))
    o = _gated_delta_chunked(q, k, v, g, beta)
    o = _rmsnorm(o, o_norm) * jax.nn.silu(z.astype(jnp.float32).reshape(B, T, DN_HEADS, DN_DV))
    return o.reshape(B, T, BRANCH_W).astype(z.dtype)


def _spatial_gating_branch(uv, z, v_gain, w_s, b_s):
    B, T, _ = uv.shape
    u, v = jnp.split(jax.nn.gelu(uv), 2, axis=-1)
    v = _rmsnorm(v, v_gain)
    n = T // GM_CHUNK
    vb = v.reshape(B, n, GM_CHUNK, GM_GROUPS, GM_CH)
    causal = jnp.tril(jnp.ones((GM_CHUNK, GM_CHUNK), dtype=bool))
    ws = jnp.where(causal, w_s, 0.0).astype(v.dtype)
    s = jnp.einsum("gpq,bnqgc->bnpgc", ws, vb) + b_s.T.astype(v.dtype)[None, None, :, :, None]
    return u * s.reshape(B, T, BRANCH_W) * jax.nn.silu(z)


def _sliding_window_branch(q, k, v, z, sinks):
    B, T, _ = q.shape
    P = SW_BLOCK
    n = T // P
    G = SW_HEADS // SW_KV_HEADS
    q = q.reshape(B, n, P, SW_KV_HEADS, G, SW_HD)
    k = k.reshape(B, n, P, SW_KV_HEADS, SW_HD)
    v = v.reshape(B, n, P, SW_KV_HEADS, SW_HD)
    pad = jnp.zeros_like(k[:, :1])
    kb = jnp.concatenate([jnp.concatenate([pad, k[:, :-1]], axis=1), k], axis=2)
    vb = jnp.concatenate([jnp.concatenate([pad, v[:, :-1]], axis=1), v], axis=2)
    s = jnp.einsum("bnqkgd,bnskd->bnkgqs", q, kb).astype(jnp.float32) * (SW_HD ** -0.5)
    qi = jnp.arange(P)[:, None]
    kj = jnp.arange(2 * P)[None, :]
    dist = qi + P - kj
    blk = jnp.arange(n)[:, None, None]
    valid = (dist >= 0) & (dist < SW_WINDOW) & (blk * P - P + kj >= 0)
    s = jnp.where(valid[None, :, None, None], s, NEG_INF)
    sink = jnp.broadcast_to(
        sinks.astype(jnp.float32).reshape(SW_KV_HEADS, G)[None, None, :, :, None, None],
        s.shape[:-1] + (1,))
    p = jax.nn.softmax(jnp.concatenate([s, sink], axis=-1), axis=-1)[..., :-1]
    o = jnp.einsum("bnkgqs,bnskd->bnqkgd", p.astype(vb.dtype), vb).reshape(B, T, BRANCH_W)
    return o * jax.nn.silu(z)


def _memory_branch(q, z, mem_kv):
    B, T, _ = q.shape
    q = q.reshape(B, T, XM_HEADS, XM_HD)
    mk, mv = jnp.split(mem_kv, 2, axis=-1)
    mk = mk.reshape(B, -1, XM_HEADS, XM_HD)
    mv = mv.reshape(B, -1, XM_HEADS, XM_HD)
    s = jnp.einsum("bthd,bmhd->bhtm", q, mk).astype(jnp.float32) * (XM_HD ** -0.5)
    p = jax.nn.softmax(s, axis=-1)
    o = jnp.einsum("bhtm,bmhd->bthd", p.astype(mv.dtype), mv).reshape(B, T, BRANCH_W)
    return o * jax.nn.silu(z)


def _layer(x, mem, norm_pre, norm_post, norm_mem, w_in, conv_w, a_log, dt_bias, dn_norm,
           gm_norm, spatial_w, spatial_b, sinks, w_mem_kv, w_up, w_out):
    B, T, D = x.shape
    h = _rmsnorm(x, norm_pre)
    cols = h @ w_in
    (dn_qkv, dn_z, dn_b, dn_a, gm_uv, gm_z, sw_q, sw_k, sw_v, sw_z,
     xm_q, xm_z, gate_logits) = _split_cols(cols)
    mem_kv = _rmsnorm(mem, norm_mem) @ w_mem_kv
    y_a = _deltanet_branch(dn_qkv, dn_z, dn_b, dn_a, conv_w, a_log, dt_bias, dn_norm)
    y_b = _spatial_gating_branch(gm_uv, gm_z, gm_norm, spatial_w, spatial_b)
    y_c = _sliding_window_branch(sw_q, sw_k, sw_v, sw_z, sinks)
    y_m = _memory_branch(xm_q, xm_z, mem_kv)
    ys = jnp.stack([y_a, y_b, y_c, y_m], axis=2)
    proj = jnp.einsum("btnc,ncd->btnd", ys, w_up)
    gates = jax.nn.sigmoid(gate_logits.reshape(B, T, N_BRANCH, D))
    merged = jnp.sum(gates * proj, axis=2)
    out = merged @ w_out
    return x + _rmsnorm(out, norm_post)


def setup_inputs(seed: int = 0) -> dict:
    key = jax.random.key(seed)
    ks = jax.random.split(key, 20)
    f32 = jnp.float32
    nrm = lambda k, shape, scale: jax.random.normal(k, shape, f32) * scale
    dt = jnp.exp(jax.random.uniform(ks[8], (DEPTH, DN_HEADS), f32,
                                    math.log(1e-3), math.log(1e-1)))
    return {
        "x": nrm(ks[0], (BATCH, SEQ, D_MODEL), 1.0),
        "mem": nrm(ks[1], (BATCH, MEM_LEN, D_MODEL), 1.0),
        "norm_pre": 1.0 + nrm(ks[2], (DEPTH, D_MODEL), 0.1),
        "norm_post": 1.0 + nrm(ks[3], (DEPTH, D_MODEL), 0.1),
        "norm_mem": 1.0 + nrm(ks[4], (DEPTH, D_MODEL), 0.1),
        "w_in": nrm(ks[5], (DEPTH, D_MODEL, D_IN), D_MODEL ** -0.5),
        "conv_w": nrm(ks[6], (DEPTH, CONV_W, 3 * BRANCH_W), CONV_W ** -0.5),
        "a_log": jnp.log(jax.random.uniform(ks[7], (DEPTH, DN_HEADS), f32, 1.0, 16.0)),
        "dt_bias": dt + jnp.log(-jnp.expm1(-dt)),
        "dn_norm": 1.0 + nrm(ks[9], (DEPTH, DN_DV), 0.1),
        "gm_norm": 1.0 + nrm(ks[10], (DEPTH, BRANCH_W), 0.1),
        "spatial_w": nrm(ks[11], (DEPTH, GM_GROUPS, GM_CHUNK, GM_CHUNK), GM_CHUNK ** -0.5),
        "spatial_b": 1.0 + nrm(ks[12], (DEPTH, GM_GROUPS, GM_CHUNK), 0.1),
        "sinks": nrm(ks[13], (DEPTH, SW_HEADS), 1.0),
        "w_mem_kv": nrm(ks[14], (DEPTH, D_MODEL, 2 * BRANCH_W), D_MODEL ** -0.5),
        "w_up": nrm(ks[15], (DEPTH, N_BRANCH, BRANCH_W, D_MODEL), BRANCH_W ** -0.5),
        "w_out": nrm(ks[16], (DEPTH, D_MODEL, D_MODEL), D_MODEL ** -0.5),
    }


def reference(x, mem, norm_pre, norm_post, norm_mem, w_in, conv_w, a_log, dt_bias, dn_norm,
              gm_norm, spatial_w, spatial_b, sinks, w_mem_kv, w_up, w_out):
    for l in range(DEPTH):
        x = _layer(x, mem, norm_pre[l], norm_post[l], norm_mem[l], w_in[l], conv_w[l],
                   a_log[l], dt_bias[l], dn_norm[l], gm_norm[l], spatial_w[l], spatial_b[l],
                   sinks[l], w_mem_kv[l], w_up[l], w_out[l])
    return x
```

```python
import contextlib
import numpy as np
import concourse.bass as bass
import concourse.mybir as mybir
from concourse.bass_utils import run_bass_kernel_spmd

F32 = mybir.dt.float32
BF16 = mybir.dt.bfloat16
AF = mybir.ActivationFunctionType
ALU = mybir.AluOpType

T = 2048
D = 1024
NT = 16
NL = 2
DIN = 9992
MEM = 256
EPS = 1e-6
ENGS = ("pe", "act", "dve", "pool", "sp")

C_ID, C_TRI, C_STRICT, C_SEL, C_ONES, C_OFF1 = 0, 1, 2, 3, 4, 5
C_OFF1T = 12
NCST = 13
P_GPRE, P_GMEM, P_CW, P_DN, P_GMN, P_SINK, P_ALOG, P_DTB = 0, 8, 16, 64, 65, 69, 73, 77
PL = 81
P_GPOST, P_SB = 0, 1024
PLB = 1024 + 512


class Buf:
    __slots__ = ("name", "w", "r", "excl")

    def __init__(self, name="", excl=False):
        self.name = name
        self.w = None
        self.r = []
        self.excl = excl


class Prog:
    def __init__(self, nc, window=10 ** 9):
        self.nc = nc
        self.ins = {e: [] for e in ENGS}
        self.seen = {e: {} for e in ENGS}
        self.dma_sems = {}
        self.window = window

    def _waits_for(self, eng, toks):
        need = {}
        my_idx = len(self.ins[eng])
        for t in toks:
            if t is None:
                continue
            k, v = t
            if k == eng:
                if eng == "pe" or eng == "sp":
                    continue
                if my_idx - v > self.window:
                    continue
            if self.seen[eng].get(k, -1) >= v:
                continue
            if need.get(k, -1) < v:
                need[k] = v
        for k, v in need.items():
            self.seen[eng][k] = v
        return list(need.items())

    def _toks(self, reads, writes):
        toks = []
        for b in reads:
            toks.append(b.w)
        for b in writes:
            toks.append(b.w)
            toks.extend(b.r)
        return toks

    def op(self, eng, fn, reads=(), writes=()):
        xr = [b for b in reads if b.excl]
        if xr:
            writes = list(writes) + [b for b in xr if b not in writes]
            reads = [b for b in reads if not b.excl]
        waits = self._waits_for(eng, self._toks(reads, writes))
        idx = len(self.ins[eng])
        self.ins[eng].append([fn, waits, False, None])
        tok = (eng, idx)
        for b in reads:
            b.r.append(tok)
        for b in writes:
            b.w = tok
            b.r = []
        return tok

    def dma(self, eng, fn, reads=(), writes=(), key=None):
        waits = self._waits_for(eng, self._toks(reads, writes))
        if key is None:
            key = id(writes[0] if writes else reads[0])
        ent = self.dma_sems.setdefault(key, [None, 0])
        ent[1] += 16
        tok = (("dma", key), ent[1])
        self.ins[eng].append([fn, waits, False, key])
        for b in reads:
            b.r.append(tok)
        for b in writes:
            b.w = tok
            b.r = []
        return tok

    def barrier(self):
        last = {}
        for e in ("pe", "act", "dve", "pool"):
            n = len(self.ins[e])
            j = n - 1
            while j >= 0 and (self.ins[e][j][3] is not None or self.ins[e][j][0] is None):
                j -= 1
            if j >= 0:
                last[e] = j
        toks = [(e, j) for e, j in last.items()]
        for key, ent in self.dma_sems.items():
            toks.append((("dma", key), ent[1]))
        for e in ("pe", "act", "dve", "pool", "sp"):
            waits = []
            for k, v in toks:
                if k == e:
                    continue
                if self.seen[e].get(k, -1) >= v:
                    continue
                self.seen[e][k] = v
                waits.append((k, v))
            if waits:
                self.ins[e].append([None, waits, False, None])

    def emit(self, final_toks=()):
        nc = self.nc
        for e in ENGS:
            for ent in self.ins[e]:
                for k, v in ent[1]:
                    if isinstance(k, str):
                        self.ins[k][v][2] = True
        for k, v in final_toks:
            if isinstance(k, str):
                self.ins[k][v][2] = True
        cnt = {}
        for e in ENGS:
            c = 0
            arr = []
            for ent in self.ins[e]:
                if ent[2]:
                    c += 1
                arr.append(c)
            cnt[e] = arr
        with contextlib.ExitStack() as st:
            esem = {e: st.enter_context(nc.semaphore("s_" + e)) for e in ENGS}
            for i, key in enumerate(self.dma_sems):
                self.dma_sems[key][0] = st.enter_context(nc.semaphore("d%d" % i))
            block = st.enter_context(nc.Block())

            def semval(k, v):
                if isinstance(k, str):
                    return esem[k], cnt[k][v]
                return self.dma_sems[k[1]][0], v

            def body(e, eng):
                for fn, waits, sig, dkey in self.ins[e]:
                    for k, v in waits:
                        s, val = semval(k, v)
                        eng.wait_ge(s, val)
                    if fn is None:
                        continue
                    inst = fn(eng)
                    if dkey is not None:
                        inst.then_inc(self.dma_sems[dkey][0], 16)
                    elif sig:
                        inst.then_inc(esem[e], 1)
                if e == "sp":
                    for k, v in final_toks:
                        s, val = semval(k, v)
                        eng.wait_ge(s, val)

            @block.tensor
            def _(eng):
                body("pe", eng)

            @block.scalar
            def _(eng):
                body("act", eng)

            @block.vector
            def _(eng):
                body("dve", eng)

            @block.gpsimd
            def _(eng):
                body("pool", eng)

            @block.sync
            def _(eng):
                body("sp", eng)


class Arena:
    def __init__(self, ap, n):
        self.ap = ap
        self.n = n
        self.off = 0

    def f32(self, n):
        assert self.off + n <= self.n, ("arena overflow", self.off, n, self.n)
        a = self.ap[:, self.off:self.off + n]
        self.off += n
        return a

    def bf16(self, n):
        return self.f32((n + 1) // 2).bitcast(BF16)[:, 0:n]


class K:
    def __init__(self, nlayers=NL, branches="abcm", dbg=None):
        self.nl = nlayers
        self.branches = branches
        self.dbg = dbg
        self.qshare = True
        nc = self.nc = bass.Bass("TRN2", target_bir_lowering=False)
        dr = lambda n, s, k="ExternalInput": nc.dram_tensor(n, s, F32, kind=k).ap()
        self.x = dr("x", [T, D])
        self.mem = dr("mem", [MEM, D])
        self.w_in = dr("w_in", [NL, D, DIN])
        self.w_mem = dr("w_mem_kv", [NL, D, 1024])
        self.w_up = dr("w_up", [NL, 4, 512, D])
        self.w_out = dr("w_out", [NL, D, D])
        self.prm_d = dr("prm", [128, NL * PL])
        self.prmb_d = dr("prmb", [NL, 128, PLB])
        self.cst_d = dr("cst", [128, NCST * 128])
        self.spw_d = dr("spw", [NL, 4, 128, 128])
        self.y = dr("y", [T, D], "ExternalOutput")
        self.xs = dr("xs", [T, D], "Internal")
        if dbg:
            self.dbg_d = dr("dbg", [dbg, 128, 2048], "ExternalOutput")

    def mm(self, out, lhsT, rhs, start, stop, reads, writes):
        self.P.op("pe", lambda e: e.matmul(out, lhsT=lhsT, rhs=rhs, start=start, stop=stop), reads, writes)

    def tr(self, out, in_, ident, reads, writes):
        self.P.op("pe", lambda e: e.transpose(out=out, in_=in_, identity=ident), reads, writes)

    def act(self, out, in_, func, reads, writes, bias=0.0, scale=1.0, accum=None):
        if accum is None:
            self.P.op("act", lambda e: e.activation(out=out, in_=in_, func=func, bias=bias, scale=scale), reads, writes)
        else:
            self.P.op("act", lambda e: e.activation(out=out, in_=in_, func=func, bias=bias, scale=scale, accum_out=accum), reads, writes)

    def tt(self, eng, out, in0, in1, op, reads, writes):
        self.P.op(eng, lambda e: e.tensor_tensor(out=out, in0=in0, in1=in1, op=op), reads, writes)

    def ts(self, eng, out, in0, s1, s2, op0, op1, reads, writes):
        if s2 is None:
            self.P.op(eng, lambda e: e.tensor_scalar(out=out, in0=in0, scalar1=s1, scalar2=None, op0=op0), reads, writes)
        else:
            self.P.op(eng, lambda e: e.tensor_scalar(out=out, in0=in0, scalar1=s1, scalar2=s2, op0=op0, op1=op1), reads, writes)

    def stt(self, eng, out, in0, scalar, in1, op0, op1, reads, writes):
        self.P.op(eng, lambda e: e.scalar_tensor_tensor(out=out, in0=in0, scalar=scalar, in1=in1, op0=op0, op1=op1), reads, writes)

    def cp(self, eng, out, in_, reads, writes):
        if eng == "act":
            self.P.op("act", lambda e: e.copy(out=out, in_=in_), reads, writes)
        else:
            self.P.op(eng, lambda e: e.tensor_copy(out=out, in_=in_), reads, writes)

    def recip(self, out, in_, reads, writes):
        self.P.op("dve", lambda e: e.reciprocal(out=out, in_=in_), reads, writes)

    def memset(self, eng, ap, val, writes):
        self.P.op(eng, lambda e: e.memset(ap, val), (), writes)

    def dma(self, eng, out, in_, reads, writes, key=None):
        self.P.dma(eng, lambda e: e.dma_start(out=out, in_=in_), reads, writes, key)

    def newps(self):
        i = self.ps_i
        self.ps_i = (i + 1) % 8
        return self.ps[i], self.psb[i]

    def cst(self, i, n=128):
        return self.cst_t[:, i * 128:i * 128 + n]

    def prm(self, l, off, n):
        return self.prm_t[:, l * PL + off:l * PL + off + n]

    def wload(self, src3, ncols, kc=8):
        i = self.ws_i
        self.ws_i = (i + 1) % len(self.ws)
        slot, b = self.ws[i], self.wsb[i]
        self.dma("pool", slot[:, 0:kc, 0:ncols], src3, (), [b])
        return slot, b

    def win(self, l, c0, n):
        return self.w_in[l][:, c0:c0 + n].rearrange("(kc p) c -> p kc c", p=128)

    def proj_fm(self, slot, sb, co, tg, extra_reads=()):
        ps, pb = self.newps()
        for kc in range(8):
            self.mm(ps[:, :], slot[:, kc, co:co + 128], self.hT[:, kc, tg * 512:(tg + 1) * 512], kc == 0, kc == 7,
                    [sb] + self.hTb[tg * 4:(tg + 1) * 4], [pb])
        return ps, pb

    def dump(self, idx, ap, reads, dt_is_bf16=False, n=2048, part=128):
        if not self.dbg:
            return
        st = self.dbg_stage
        b = self.dbg_stage_b
        self.cp("dve", st[0:part, 0:n], ap, reads, [b])
        self.dma("sp", self.dbg_d[idx][0:part, 0:n], st[0:part, 0:n], [b], [self.dbg_out_b], key="dbg")

    def build(self):
        nc = self.nc
        with contextlib.ExitStack() as st:
            self.st = st
            sb = lambda n, s, d=F32: st.enter_context(nc.sbuf_tensor("sb_" + n, s, d))
            self.P = Prog(nc)
            self.ps = [st.enter_context(nc.psum_tensor("ps%d" % i, [128, 512], F32)) for i in range(8)]
            self.psb = [Buf("ps%d" % i, excl=True) for i in range(8)]
            self.ps_i = 0
            self.cst_t = sb("cst", [128, NCST * 128])
            self.prm_t = sb("prm", [128, NL * PL])
            self.cstb = Buf("cst")
            self.prmb = Buf("prm")
            self.prmB_t = sb("prmB", [128, PLB])
            self.prmBb = Buf("prmB")
            self.hT = sb("hT", [128, 8, T], BF16)
            self.hTb = [Buf("hT%d" % i) for i in range(NT)]
            self.ws = [sb("ws%d" % i, [128, 8, 512], BF16) for i in range(3 if self.dbg else 4)]
            self.wsb = [Buf("ws%d" % i) for i in range(len(self.ws))]
            self.ws_i = 0
            nslots = 3 if self.dbg else 4
            self.TAIL = 12300
            self.big = sb("big", [128, 12288 + self.TAIL])
            self.yT = [sb("yT0", [128, 4, T], BF16)] + [
                self.big[:, i * 4096:(i + 1) * 4096].bitcast(BF16).rearrange("p (c t) -> p c t", c=4) for i in range(3)]
            self.yTb = [[[Buf() for tg in range(4)] for c in range(4)] for i in range(4)]
            self.cb16 = sb("cb16", [128, 5 * 128], BF16)
            self.cb16b = Buf("cb16")
            self.xin = [sb("xin%d" % i, [128, D]) for i in range(2)]
            self.xinb = [Buf() for i in range(2)]
            self.xsc = sb("xsc", [128, D])
            self.xscb = Buf()
            self.stat = sb("stat", [128, 64])
            self.statb = Buf()
            self.memT = sb("memT", [128, 8, MEM], BF16)
            self.memTb = Buf()
            self.arena_t = self.big[:, 12288:12288 + self.TAIL]
            self.ARN = self.TAIL
            if self.dbg:
                self.dbg_stage = sb("dbgst", [128, 2048])
                self.dbg_stage_b = Buf()
                self.dbg_out_b = Buf()
            self.out_b = Buf("out")

            self.dma("sp", self.cst_t[:, :], self.cst_d[:, :], (), [self.cstb])
            self.dma("sp", self.prm_t[:, :], self.prm_d[:, :], (), [self.prmb])
            self.cp("act", self.cb16[:, 0:128], self.cst(C_ID), [self.cstb], [self.cb16b])
            self.cp("act", self.cb16[:, 128:256], self.cst(C_TRI), [self.cstb], [self.cb16b])
            self.cp("act", self.cb16[:, 256:384], self.cst(C_ONES), [self.cstb], [self.cb16b])
            self.memset("dve", self.cb16[:, 384:576], 0.0, [self.cb16b])
            self.cp("act", self.cb16[:, 448:512], self.cst(C_ONES, 64), [self.cstb], [self.cb16b])
            self.identb = self.cb16[:, 0:128]
            self.trib = self.cb16[:, 128:256]
            self.onesb = self.cb16[:, 256:384]
            self.onesO = self.cb16[:, 384:512]
            self.onesE = self.cb16[:, 448:576]

            for t in range(NT):
                xi, xb = self.xin[t % 2], self.xinb[t % 2]
                self.dma("sp", xi[:, :], self.x[t * 128:(t + 1) * 128, :], (), [xb])
                self.norm_T(xi, xb, 0, t)
            for l in range(self.nl):
                self.layer(l)
            self.P.emit(final_toks=[self.out_b.w])
        return nc

    def norm_T(self, xi, xb, l, t):
        ssq = self.stat[:, t:t + 1]
        rs = self.stat[:, 32 + t:33 + t]
        self.memset("dve", ssq, 0.0, [self.statb])
        self.act(self.xsc[:, :], xi[:, :], AF.Square, [xb, self.statb], [self.xscb, self.statb], accum=ssq)
        self.rstd(rs, ssq, 1.0 / D)
        self.act(self.xsc[:, :], xi[:, :], AF.Copy, [xb, self.statb], [self.xscb], scale=rs)
        for half in range(2):
            ps, pb = self.newps()
            for j in range(4):
                kc = half * 4 + j
                self.tr(ps[:, j * 128:(j + 1) * 128], self.xsc[:, kc * 128:(kc + 1) * 128], self.cst(C_ID), [self.xscb, self.cstb], [pb])
            g = self.prm(l, P_GPRE + half * 4, 4)
            self.tt("dve", self.hT[:, half * 4:half * 4 + 4, t * 128:(t + 1) * 128], ps[:, :].rearrange("p (a b) -> p a b", a=4),
                    g.unsqueeze(2).to_broadcast([128, 4, 128]), ALU.mult, [pb, self.prmb], [self.hTb[t]])

    def rstd(self, out, ssq, scale, eps=EPS):
        self.act(out, ssq, AF.Ln, [self.statb], [self.statb], bias=eps, scale=scale)
        self.act(out, out, AF.Exp, [self.statb], [self.statb], scale=-0.5)

    def layer(self, l):
        P = self.P
        self.dma("sp", self.prmB_t[:, :], self.prmb_d[l], (), [self.prmBb])
        if "a" in self.branches:
            self.branch_a(l)
        if "b" in self.branches:
            self.branch_b(l)
        if "c" in self.branches:
            self.branch_c(l)
        if "m" in self.branches:
            self.branch_m(l)
        P.barrier()
        for i in range(4):
            if "abcm"[i] not in self.branches:
                for c in range(4):
                    self.memset("dve", self.yT[i][:, c, :], 0.0, [b for b in self.yTb[i][c]])
        self.merge_out(l)


    def branch_a(self, l):
        P = self.P
        P.barrier()
        ar = Arena(self.big[:, 0:12288 + self.TAIL], 12288 + self.TAIL)
        qT = ar.bf16(4 * T).rearrange("p (c t) -> p c t", c=4)
        kT = ar.bf16(4 * T).rearrange("p (c t) -> p c t", c=4)
        qkb = {("q", h): [Buf() for _ in range(4)] for h in range(4)}
        qkb.update({("k", h): [Buf() for _ in range(4)] for h in range(4)})
        ktok = ar.bf16(16 * 512).rearrange("p (t h d) -> p t h d", t=16, h=4)
        vtok = ar.bf16(16 * 512).rearrange("p (t h d) -> p t h d", t=16, h=4)
        ktokb = [[Buf() for _ in range(2)] for _ in range(4)]
        vtokb = [[Buf() for _ in range(2)] for _ in range(4)]
        gt = {n_: ar.f32(64) for n_ in ("g", "beta", "gc", "ngc", "egc", "begc", "ekd", "glb", "dl", "tmp")}
        ea = ar.f32(4)
        ghl = ar.bf16(128)
        ghi, glo = ghl[:, 0:64], ghl[:, 64:128]
        gb = Buf("gates")
        mark = ar.off
        raw = ar.f32(2052)
        rawb = Buf()
        acc = ar.f32(T)
        accb = [Buf(), Buf()]
        sq = ar.bf16(T)
        sqb = Buf()
        vT = ar.bf16(T)
        vTb = Buf()
        rn = [ar.f32(512) for _ in range(2)]
        rnb = [Buf(), Buf()]
        self.memset("dve", raw[:, 0:4], 0.0, [rawb])
        slot = sbf = None
        for ci in range(12):
            kind, h = "qkv"[ci // 4], ci % 4
            if ci % 4 == 0:
                slot, sbf = self.wload(self.win(l, ci * 128, 512), 512)
            for tg in range(4):
                ps, pb = self.proj_fm(slot, sbf, (ci % 4) * 128, tg)
                self.cp("act", raw[:, 3 + tg * 512:3 + (tg + 1) * 512], ps[:, :], [pb], [rawb])
            for hf, eng in ((0, "dve"), (1, "dve")):
                t0 = hf * 1024
                cw = lambda k_: self.prm(l, P_CW + ci * 4 + k_, 1)
                self.ts(eng, acc[:, t0:t0 + 1024], raw[:, 3 + t0:3 + t0 + 1024], cw(3), None, ALU.mult, None, [rawb, self.prmb], [accb[hf]])
                for k_ in (2, 1, 0):
                    self.stt(eng, acc[:, t0:t0 + 1024], raw[:, k_ + t0:k_ + t0 + 1024], cw(k_), acc[:, t0:t0 + 1024], ALU.mult, ALU.add,
                             [rawb, self.prmb, accb[hf]], [accb[hf]])
            if kind == "v":
                self.act(vT[:, :], acc[:, :], AF.Silu, accb, [vTb])
                src, srcb, dst, dstb = vT, [vTb], vtok, vtokb[h]
            else:
                self.act(acc[:, :], acc[:, :], AF.Silu, accb, accb)
                self.act(sq[:, :], acc[:, :], AF.Square, accb, [sqb])
                dT = qT if kind == "q" else kT
                sc = 128.0 if kind == "q" else 1.0
                for tg in range(4):
                    k = tg % 2
                    ps, pb = self.newps()
                    self.mm(ps[:, :], self.onesb, sq[:, tg * 512:(tg + 1) * 512], True, True, [self.cb16b, sqb], [pb])
                    self.act(rn[k], ps[:, :], AF.Ln, [pb], [rnb[k]], bias=EPS * sc, scale=sc)
                    self.act(rn[k], rn[k], AF.Exp, [rnb[k]], [rnb[k]], scale=-0.5)
                    self.tt("dve", dT[:, h, tg * 512:(tg + 1) * 512], acc[:, tg * 512:(tg + 1) * 512], rn[k], ALU.mult, [accb[tg // 2], rnb[k]], [qkb[(kind, h)][tg]])
                if kind == "k":
                    src, srcb, dst, dstb = kT[:, h, :], None, ktok, ktokb[h]
            if kind in "kv":
                for half in range(2):
                    ps, pb = self.newps()
                    psv = ps[:, :].bitcast(BF16)
                    for j in range(8):
                        t = half * 8 + j
                        rd = [vTb] if kind == "v" else [qkb[("k", h)][t // 4]]
                        self.tr(psv[:, j * 128:(j + 1) * 128], src[:, t * 128:(t + 1) * 128], self.identb, rd + [self.cb16b], [pb])
                    self.cp("act", dst[:, half * 8:half * 8 + 8, h, :], psv[:, :].rearrange("p (a b) -> p a b", a=8), [pb], [dstb[half]])
        gslot, gsb = self.wload(self.win(l, 2048, 8), 8)
        psg, psgb = self.newps()
        for t in range(NT):
            for kc in range(8):
                self.mm(psg[:, t * 8:(t + 1) * 8], self.hT[:, kc, t * 128:(t + 1) * 128], gslot[:, kc, 0:8], kc == 0, kc == 7, [gsb, self.hTb[t]], [psgb])
        pv = psg[:, 0:128].rearrange("p (t j) -> p t j", t=16)
        v3 = lambda a: a.rearrange("p (t j) -> p t j", t=16)
        self.act(v3(gt["beta"]), pv[:, :, 0:4], AF.Sigmoid, [psgb], [gb])
        self.tt("dve", v3(gt["tmp"]), pv[:, :, 4:8], self.prm(l, P_DTB, 4).unsqueeze(1).to_broadcast([128, 16, 4]), ALU.add, [psgb, self.prmb], [gb])
        self.act(gt["tmp"], gt["tmp"], AF.Exp, [gb], [gb])
        self.act(gt["tmp"], gt["tmp"], AF.Ln, [gb], [gb], bias=1.0)
        self.act(ea, self.prm(l, P_ALOG, 4), AF.Exp, [self.prmb], [gb])
        self.stt("dve", v3(gt["g"]), v3(gt["tmp"]), -1.0, ea.unsqueeze(1).to_broadcast([128, 16, 4]), ALU.mult, ALU.mult, [gb], [gb])
        ps, pb = self.newps()
        self.mm(ps[:, 0:64], self.cst(C_TRI), gt["g"], True, True, [self.cstb, gb], [pb])
        self.cp("dve", gt["gc"], ps[:, 0:64], [pb], [gb])
        ps, pb = self.newps()
        self.mm(ps[:, 0:64], self.cst(C_SEL), gt["gc"], True, True, [self.cstb, gb], [pb])
        self.cp("dve", gt["glb"], ps[:, 0:64], [pb], [gb])
        self.act(gt["egc"], gt["gc"], AF.Exp, [gb], [gb])
        self.act(gt["dl"], gt["glb"], AF.Exp, [gb], [gb])
        self.ts("dve", gt["ngc"], gt["gc"], -1.0, None, ALU.mult, None, [gb], [gb])
        self.tt("dve", gt["ekd"], gt["glb"], gt["gc"], ALU.subtract, [gb], [gb])
        self.act(gt["ekd"], gt["ekd"], AF.Exp, [gb], [gb])
        self.tt("dve", gt["begc"], gt["beta"], gt["egc"], ALU.mult, [gb], [gb])
        self.cp("dve", ghi, gt["g"], [gb], [gb])
        self.cp("dve", gt["tmp"], ghi, [gb], [gb])
        self.tt("dve", gt["tmp"], gt["g"], gt["tmp"], ALU.subtract, [gb], [gb])
        self.cp("dve", glo, gt["tmp"], [gb], [gb])
        P.barrier()
        ar.off = mark
        extra = [Arena(self.xin[0][:, :], 1024), Arena(self.xin[1][:, :], 1024), Arena(self.xsc[:, :], 1024),
                 Arena(self.memT[:, :, :].rearrange("p a b -> p (a b)").bitcast(F32), 1024)]

        def alloc(n_f32):
            for a_ in [ar] + extra:
                if a_.off + n_f32 <= a_.n:
                    return a_.f32(n_f32)
            raise AssertionError("A3 scratch overflow")

        def mkset(fnames, bnames):
            d = {}
            for nm in fnames:
                d[nm] = alloc(128)
            for nm in bnames:
                d[nm] = alloc(64).bitcast(BF16)
            d["b"] = {nm: Buf() for nm in d.keys()}
            return d
        Sf = ar.f32(512).rearrange("p (h d) -> p h d", h=4)
        Sb = ar.bf16(512).rearrange("p (h d) -> p h d", h=4)
        Sfb = [Buf() for _ in range(4)]
        Sbb = [Buf() for _ in range(4)]
        NSC, NOP = 4, 8
        scr = [mkset(("D1", "D2", "L", "AT", "X0", "X1", "XT0", "XT1"), ("ATb", "Xb", "XTs", "Yb")) for _ in range(NSC)]
        for d in scr:
            d["Y"] = d["L"]
            d["b"]["Y"] = d["b"]["L"]
        ops = [mkset((), ("XTb", "nwT", "kbg", "vb", "kd", "aqkT", "vnb")) for _ in range(NOP)]
        scb = []
        for _ in range(4):
            d = mkset(("t1", "o"), ("onb",))
            d["st"] = alloc(4)
            d["b"]["st"] = Buf()
            scb.append(d)
        for h in range(4):
            self.memset("dve", Sf[:, h, :], 0.0, [Sfb[h]])
            self.memset("dve", Sb[:, h, :], 0.0, [Sbb[h]])
        ID, TRI, STR = self.cst(C_ID), self.cst(C_TRI), self.cst(C_STRICT)
        qbank = [Buf() for _ in range(8)]
        qps = [(self.ps[i // 4][:, (i % 4) * 128:(i % 4 + 1) * 128], qbank[i // 4] if self.qshare else Buf()) for i in range(32)]
        qi = [0]

        def newq():
            bk_, bB_ = self.newps()
            return (bk_[:, 0:128], bB_)

        def inv_gen(n, h, d, o):
            B, OB = d["b"], o["b"]
            ck = slice(n * 128, (n + 1) * 128)
            j = n * 4 + h
            col = lambda nm: gt[nm][:, j:j + 1]
            kb_, qb_ = qkb[("k", h)][n // 4], qkb[("q", h)][n // 4]
            bk_, bB_ = self.ps[h], self.psb[h]
            newq = lambda: (bk_[:, 384:512], bB_)
            (KK, bKK), (QKT, bQK), (GCR, bG) = (bk_[:, 0:128], bB_), (bk_[:, 128:256], bB_), (bk_[:, 256:384], bB_)
            self.mm(KK, kT[:, h, ck], kT[:, h, ck], True, True, [kb_], [bKK])
            self.mm(QKT, kT[:, h, ck], qT[:, h, ck], True, True, [kb_, qb_], [bQK])
            self.mm(GCR, ghi[:, j:j + 1].to_broadcast([128, 128]), self.trib, True, False, [gb, self.cb16b], [bG])
            self.mm(GCR, glo[:, j:j + 1].to_broadcast([128, 128]), self.trib, False, True, [gb, self.cb16b], [bG])
            yield
            self.act(d["D1"], GCR, AF.Exp, [bG, gb], [B["D1"]], bias=col("gc"), scale=-1.0)
            self.act(d["D2"], GCR, AF.Exp, [bG, gb], [B["D2"]], bias=col("ngc"), scale=1.0)
            yield
            self.stt("dve", d["D1"], d["D1"], 1.0, STR, ALU.min, ALU.mult, [B["D1"], self.cstb], [B["D1"]])
            self.stt("dve", d["D2"], d["D2"], 1.0, TRI, ALU.min, ALU.mult, [B["D2"], self.cstb], [B["D2"]])
            yield
            self.stt("dve", d["L"], KK, col("beta"), d["D1"], ALU.mult, ALU.mult, [bKK, gb, B["D1"]], [B["L"]])
            self.tt("dve", o["aqkT"], QKT, d["D2"], ALU.mult, [bQK, B["D2"]], [OB["aqkT"]])
            yield
            (pt_, bpt) = newq()
            self.tr(pt_, d["L"], ID, [B["L"], self.cstb], [bpt])
            self.tt("pool", d["X0"], d["L"], self.cst(C_OFF1), ALU.mult, [B["L"], self.cstb], [B["X0"]])
            yield
            self.tt("dve", d["AT"], pt_, ID, ALU.add, [bpt, self.cstb], [B["AT"]])
            self.tt("pool", d["X0"], ID, d["X0"], ALU.subtract, [B["X0"], self.cstb], [B["X0"]])
            yield
            self.cp("act", d["ATb"], d["AT"], [B["AT"]], [B["ATb"]])
            self.tt("pool", d["XT0"], d["AT"], self.cst(C_OFF1T), ALU.mult, [B["AT"], self.cstb], [B["XT0"]])
            self.cp("pool", d["Xb"], d["X0"], [B["X0"]], [B["Xb"]])
            yield
            self.tt("pool", d["XT0"], ID, d["XT0"], ALU.subtract, [B["XT0"], self.cstb], [B["XT0"]])
            yield
            self.cp("pool", d["XTs"], d["XT0"], [B["XT0"]], [B["XTs"]])
            yield
            cur = 0
            for lv in range(2, 8):
                X, XT, Xn, XTn = d["X%d" % cur], d["XT%d" % cur], d["X%d" % (1 - cur)], d["XT%d" % (1 - cur)]
                bX, bXT, bXn, bXTn = B["X%d" % cur], B["XT%d" % cur], B["X%d" % (1 - cur)], B["XT%d" % (1 - cur)]
                (pY, bpY) = (bk_[:, 0:128], bB_)
                self.mm(pY, d["ATb"], d["Xb"], True, True, [B["ATb"], B["Xb"]], [bpY])
                yield
                self.tt("dve", d["Yb"], pY, self.cst(C_OFF1 + lv - 1), ALU.mult, [bpY, self.cstb], [B["Yb"]])
                yield
                if lv < 7:
                    (pZ, bpZ) = (bk_[:, 128:256], bB_)
                    self.mm(pZ, d["XTs"], d["Yb"], True, True, [B["XTs"], B["Yb"]], [bpZ])
                (pZT, bpZT) = (bk_[:, 256:384], bB_)
                self.mm(pZT, d["Yb"], d["XTs"], True, True, [B["XTs"], B["Yb"]], [bpZT])
                yield
                if lv < 7:
                    self.tt("dve", Xn, X, pZ, ALU.subtract, [bpZ, bX], [bXn])
                self.tt("dve", XTn, XT, pZT, ALU.subtract, [bpZT, bXT], [bXTn])
                yield
                if lv < 7:
                    self.cp("pool", d["Xb"], Xn, [bXn], [B["Xb"]])
                    self.cp("act", d["XTs"], XTn, [bXTn], [B["XTs"]])
                    yield
                cur = 1 - cur
            XTf, bXTf = d["XT%d" % cur], B["XT%d" % cur]
            self.cp("act", o["XTb"], XTf, [bXTf], [OB["XTb"]])
            self.act(o["kbg"], ktok[:, n, h, :], AF.Copy, [ktokb[h][n // 8], gb], [OB["kbg"]], scale=col("begc"))
            self.act(o["vb"], vtok[:, n, h, :], AF.Copy, [vtokb[h][n // 8], gb], [OB["vb"]], scale=col("beta"))
            self.act(o["kd"], ktok[:, n, h, :], AF.Copy, [ktokb[h][n // 8], gb], [OB["kd"]], scale=col("ekd"))
            yield
            (pw, bpw) = newq()
            self.mm(pw, o["kbg"], o["XTb"], True, True, [OB["kbg"], OB["XTb"]], [bpw])
            yield
            self.act(o["nwT"], pw, AF.Copy, [bpw], [OB["nwT"]], scale=-1.0)
            yield

        def scan_gen(n, h, o, d):
            B, OB = d["b"], o["b"]
            ck = slice(n * 128, (n + 1) * 128)
            j = n * 4 + h
            col = lambda nm: gt[nm][:, j:j + 1]
            qb_ = qkb[("q", h)][n // 4]
            bk_, bB_ = self.ps[4 + h], self.psb[4 + h]
            newq = lambda: (bk_[:, 0:128], bB_)
            (VN, bVN), (P1, bP1) = (bk_[:, 0:128], bB_), (bk_[:, 128:256], bB_)
            self.mm(VN, o["XTb"], o["vb"], True, False, [OB["XTb"], OB["vb"]], [bVN])
            self.mm(VN, o["nwT"], Sb[:, h, :], False, True, [OB["nwT"], Sbb[h]], [bVN])
            self.mm(P1, qT[:, h, ck], Sb[:, h, :], True, True, [qb_, Sbb[h]], [bP1])
            yield
            self.cp("act", o["vnb"], VN, [bVN], [OB["vnb"]])
            yield
            (P2, bP2), (SU, bSU) = (bk_[:, 256:384], bB_), (bk_[:, 384:512], bB_)
            self.mm(SU, o["kd"], o["vnb"], True, True, [OB["kd"], OB["vnb"]], [bSU])
            self.mm(P2, o["aqkT"], o["vnb"], True, True, [OB["aqkT"], OB["vnb"]], [bP2])
            self.act(d["t1"], P1, AF.Copy, [bP1, gb], [B["t1"]], scale=col("egc"))
            yield
            self.stt("dve", Sf[:, h, :], Sf[:, h, :], col("dl"), SU, ALU.mult, ALU.add, [Sfb[h], gb, bSU], [Sfb[h]])
            yield
            self.cp("act", Sb[:, h, :], Sf[:, h, :], [Sfb[h]], [Sbb[h]])
            self.tt("dve", d["onb"], d["t1"], P2, ALU.add, [B["t1"], bP2], [B["onb"]])
            yield
            (pt_, bpt) = newq()
            ptv = pt_.bitcast(BF16)
            self.tr(ptv[:, 0:128], d["onb"], self.identb, [B["onb"], self.cb16b], [bpt])
            yield
            self.cp("act", self.yT[0][:, h, ck], ptv[:, 0:128], [bpt], [self.yTb[0][h][n // 4]])
            yield

        import os
        GW = int(os.environ.get("A3_G", "8"))

        def run(gens_all):
            gens_all = list(gens_all)
            for g0 in range(0, len(gens_all), GW):
                run1(gens_all[g0:g0 + GW])

        def run1(gens):
            gens = list(gens)
            while gens:
                nxt = []
                for g_ in gens:
                    try:
                        next(g_)
                        nxt.append(g_)
                    except StopIteration:
                        pass
                gens = nxt

        NMAX = int(os.environ.get("A3_NMAX", str(NT)))
        NOSCAN = int(os.environ.get("A3_NOSCAN", "0"))
        run([inv_gen(0, h, scr[h], ops[h]) for h in range(4)])
        for n in range(NMAX):
            gens = [scan_gen(n, h, ops[(n % 2) * 4 + h], scb[h]) for h in range(4)] if not NOSCAN else []
            if n + 1 < NMAX:
                gens += [inv_gen(n + 1, h, scr[h], ops[((n + 1) % 2) * 4 + h]) for h in range(4)]
            run(gens)
        P.barrier()
        ar.off = mark
        szs = [ar.f32(512) for _ in range(2)]
        szb = [Buf(), Buf()]
        sq2 = [ar.bf16(512) for _ in range(2)]
        sq2b = [Buf(), Buf()]
        rn2 = [ar.f32(512) for _ in range(2)]
        rn2b = [Buf(), Buf()]
        zslot, zsb = self.wload(self.win(l, 1536, 512), 512)
        it = 0
        for c in range(4):
            for tg in range(4):
                k = it % 2
                it += 1
                ysl = self.yT[0][:, c, tg * 512:(tg + 1) * 512]
                yb_ = self.yTb[0][c][tg]
                self.act(sq2[k], ysl, AF.Square, [yb_], [sq2b[k]])
                ps, pb = self.newps()
                self.mm(ps[:, :], self.onesb, sq2[k], True, True, [self.cb16b, sq2b[k]], [pb])
                self.act(rn2[k], ps[:, :], AF.Ln, [pb], [rn2b[k]], bias=EPS, scale=1.0 / 128)
                self.act(rn2[k], rn2[k], AF.Exp, [rn2b[k]], [rn2b[k]], scale=-0.5)
                self.tt("dve", ysl, ysl, rn2[k], ALU.mult, [yb_, rn2b[k]], [yb_])
        it = 0
        for c in range(4):
            for tg in range(4):
                k = it % 2
                it += 1
                pz, pzb = self.proj_fm(zslot, zsb, c * 128, tg)
                self.act(szs[k], pz[:, :], AF.Silu, [pzb], [szb[k]])
                self.stt("dve", self.yT[0][:, c, tg * 512:(tg + 1) * 512], self.yT[0][:, c, tg * 512:(tg + 1) * 512], self.prm(l, P_DN, 1), szs[k],
                         ALU.mult, ALU.mult, [self.yTb[0][c][tg], szb[k], self.prmb], [self.yTb[0][c][tg]])
        if self.dbg == 12 and l == 0 and self.branches[-1] == "a":
            for h in range(4):
                self.dump(h, self.yT[0][:, h, :], self.yTb[0][h])

    def branch_b(self, l):
        P = self.P
        P.barrier()
        ar = Arena(self.big[:, 4096:12288 + self.TAIL], 8192 + self.TAIL)
        vtok = ar.bf16(16 * 512).rearrange("p (t c) -> p t c", t=16)
        vtokb = [Buf() for _ in range(16)]
        uT = ar.bf16(4 * T).rearrange("p (c t) -> p c t", c=4)
        uTb = [[Buf() for _ in range(4)] for _ in range(4)]
        wsT = ar.bf16(512).rearrange("p (g q) -> p g q", g=4)
        wsTb = Buf()
        wtmp = ar.f32(512).rearrange("p (g q) -> p g q", g=4)
        wtmpb = Buf()
        gv = [ar.f32(512) for _ in range(2)]
        gvb = [Buf(), Buf()]
        junk = ar.f32(512)
        junkb = Buf()
        ssb = [ar.f32(512) for _ in range(2)]
        ssbb = [Buf(), Buf()]
        sz = [ar.f32(512) for _ in range(2)]
        szb = [Buf(), Buf()]
        self.dma("sp", wtmp[:, :, :], self.spw_d[l].rearrange("g q p -> q g p"), (), [wtmpb])
        self.tt("dve", wsT[:, :, :], wtmp[:, :, :], self.cst(C_TRI).unsqueeze(1).to_broadcast([128, 4, 128]), ALU.mult, [wtmpb, self.cstb], [wsTb])
        slot, sbf = self.wload(self.win(l, 2568, 512), 512)
        for t in range(NT):
            k = t % 2
            ps, pb = self.newps()
            for kc in range(8):
                self.mm(ps[:, :], self.hT[:, kc, t * 128:(t + 1) * 128], slot[:, kc, :], kc == 0, kc == 7, [sbf, self.hTb[t]], [pb])
            self.act(gv[k], ps[:, :], AF.Gelu_apprx_tanh, [pb], [gvb[k]])
            ssq = self.stat[:, t:t + 1]
            rs = self.stat[:, 32 + t:33 + t]
            self.memset("dve", ssq, 0.0, [self.statb])
            self.act(junk, gv[k], AF.Square, [gvb[k], self.statb], [junkb, self.statb], accum=ssq)
            self.rstd(rs, ssq, 1.0 / 512)
            self.act(vtok[:, t, :], gv[k], AF.Copy, [gvb[k], self.statb], [vtokb[t]], scale=rs)
        slot, sbf = self.wload(self.win(l, 2056, 512), 512)
        for c in range(4):
            for tg in range(4):
                ps, pb = self.proj_fm(slot, sbf, c * 128, tg)
                self.act(uT[:, c, tg * 512:(tg + 1) * 512], ps[:, :], AF.Gelu_apprx_tanh, [pb], [uTb[c][tg]])
        zslot, zsb = self.wload(self.win(l, 3080, 512), 512)
        it = 0
        for g in range(4):
            for tg in range(4):
                k = it % 2
                it += 1
                ps, pb = self.newps()
                for j in range(4):
                    n = tg * 4 + j
                    self.mm(ps[:, j * 128:(j + 1) * 128], vtok[:, n, g * 128:(g + 1) * 128], wsT[:, g, :], True, True, [vtokb[n], wsTb], [pb])
                sbt = self.prmB_t[:, P_SB + g * 128:P_SB + (g + 1) * 128]
                self.stt("dve", ssb[k].rearrange("p (a b) -> p a b", a=4), ps[:, :].rearrange("p (a b) -> p a b", a=4),
                         self.prm(l, P_GMN + g, 1), sbt.unsqueeze(1).to_broadcast([128, 4, 128]), ALU.mult, ALU.add,
                         [pb, self.prmb, self.prmBb], [ssbb[k]])
                pz, pzb = self.proj_fm(zslot, zsb, g * 128, tg)
                self.act(sz[k], pz[:, :], AF.Silu, [pzb], [szb[k]])
                self.tt("dve", ssb[k], ssb[k], uT[:, g, tg * 512:(tg + 1) * 512], ALU.mult, [ssbb[k], uTb[g][tg]], [ssbb[k]])
                self.tt("dve", self.yT[1][:, g, tg * 512:(tg + 1) * 512], ssb[k], sz[k], ALU.mult, [ssbb[k], szb[k]], [self.yTb[1][g][tg]])
        if self.dbg == 12 and l == 0 and self.branches[-1] == "b":
            for h in range(4):
                self.dump(h, self.yT[1][:, h, :], self.yTb[1][h])

    def branch_c(self, l):
        P = self.P
        P.barrier()
        ar = Arena(self.big[:, 8192:12288 + self.TAIL], 4096 + self.TAIL)
        qT = ar.bf16(4 * T).rearrange("p (c t) -> p c t", c=4)
        qTb = [[Buf() for _ in range(4)] for _ in range(4)]
        kp = ar.bf16(4 * T).rearrange("p (c t) -> p c t", c=4)
        kpb = [[Buf() for _ in range(4)] for _ in range(4)]
        vz = ar.bf16(16 * 2 * 192).rearrange("p (t g d) -> p t g d", t=16, g=2)
        vzb = [Buf() for _ in range(16)]
        es = ar.f32(4)
        esb = Buf()
        pT = [[ar.bf16(512) for kb in range(2)] for _ in range(2)]
        pTb = [[Buf(), Buf()], [Buf(), Buf()]]
        rden = [ar.f32(256) for _ in range(2)]
        rdenb = [Buf(), Buf()]
        sz = [ar.f32(512) for _ in range(2)]
        szb = [Buf(), Buf()]
        self.act(es, self.prm(l, P_SINK, 4), AF.Exp, [self.prmb], [esb])
        self.memset("dve", vz[:, :, :, :], 0.0, vzb)
        slot, sbf = self.wload(self.win(l, 3592, 512), 512)
        for c in range(4):
            for tg in range(4):
                ps, pb = self.proj_fm(slot, sbf, c * 128, tg)
                self.cp("act", qT[:, c, tg * 512:(tg + 1) * 512], ps[:, :], [pb], [qTb[c][tg]])
        i = self.ws_i
        self.ws_i = (i + 1) % len(self.ws)
        slot, sbf = self.ws[i], self.wsb[i]
        self.memset("pool", slot[:, :, :], 0.0, [sbf])
        for (co, src) in ((0, 4104), (192, 4104), (256, 4168), (448, 4168)):
            self.dma("pool", slot[:, :, co:co + 64], self.win(l, src, 64), (), [sbf])
        for j in range(4):
            for tg in range(4):
                ps, pb = self.proj_fm(slot, sbf, j * 128, tg)
                self.cp("act", kp[:, j, tg * 512:(tg + 1) * 512], ps[:, :], [pb], [kpb[j][tg]])
        slot, sbf = self.wload(self.win(l, 4232, 128), 128)
        for t in range(NT):
            ps, pb = self.newps()
            for kc in range(8):
                self.mm(ps[:, 0:128], self.hT[:, kc, t * 128:(t + 1) * 128], slot[:, kc, 0:128], kc == 0, kc == 7, [sbf, self.hTb[t]], [pb])
            self.cp("act", vz[:, t, :, 64:128], ps[:, 0:128].rearrange("p (g d) -> p g d", g=2), [pb], [vzb[t]])
        it = 0
        for n in range(NT):
            tb = slice(n * 128, (n + 1) * 128)
            for g in range(2):
                k = it % 2
                it += 1
                kbs = [1] if n == 0 else [0, 1]
                for kb in kbs:
                    blk = n - 1 + kb
                    sbk = slice(blk * 128, (blk + 1) * 128)
                    ps, pb = self.newps()
                    for par in range(2):
                        self.mm(ps[:, par * 256:(par + 1) * 256].rearrange("p (a b) -> p a b", a=2), kp[:, g * 2 + par, sbk], qT[:, 2 * g:2 * g + 2, tb],
                                True, True, [kpb[g * 2 + par][blk // 4], qTb[2 * g][n // 4], qTb[2 * g + 1][n // 4]], [pb])
                    self.act(pT[k][kb], ps[:, :], AF.Exp, [pb], [pTb[k][kb]], scale=0.125)
                    msk = self.cst(C_TRI) if kb == 1 else self.cst(C_STRICT)
                    self.tt("dve", pT[k][kb].rearrange("p (a b) -> p a b", a=4), pT[k][kb].rearrange("p (a b) -> p a b", a=4),
                            msk.unsqueeze(1).to_broadcast([128, 4, 128]), ALU.mult, [pTb[k][kb], self.cstb], [pTb[k][kb]])
                po, pob = self.newps()
                combos = [(kb, par) for kb in kbs for par in range(2)]
                for ci, (kb, par) in enumerate(combos):
                    blk = n - 1 + kb
                    lhs = vz[:, blk, g, 64:192] if par == 0 else vz[:, blk, g, 0:128]
                    self.mm(po[:, 0:256], lhs, pT[k][kb][:, par * 256:(par + 1) * 256], ci == 0, ci == len(combos) - 1, [vzb[blk], pTb[k][kb]], [pob])
                for ci, (kb, par) in enumerate(combos):
                    lhs = self.onesE if par == 0 else self.onesO
                    self.mm(po[:, 256:512], lhs, pT[k][kb][:, par * 256:(par + 1) * 256], ci == 0, ci == len(combos) - 1, [self.cb16b, pTb[k][kb]], [pob])
                self.tt("dve", rden[k].rearrange("p (a b) -> p a b", a=2), po[:, 256:512].rearrange("p (a b) -> p a b", a=2),
                        es[:, 2 * g:2 * g + 2].unsqueeze(2).to_broadcast([128, 2, 128]), ALU.add, [pob, esb], [rdenb[k]])
                self.act(rden[k], rden[k], AF.Ln, [rdenb[k]], [rdenb[k]])
                self.act(rden[k], rden[k], AF.Exp, [rdenb[k]], [rdenb[k]], scale=-1.0)
                self.tt("dve", self.yT[2][:, 2 * g:2 * g + 2, tb], po[:, 0:256].rearrange("p (a b) -> p a b", a=2), rden[k].rearrange("p (a b) -> p a b", a=2),
                        ALU.mult, [pob, rdenb[k]], [self.yTb[2][2 * g][n // 4], self.yTb[2][2 * g + 1][n // 4]])
        zslot, zsb = self.wload(self.win(l, 4360, 512), 512)
        it = 0
        for c in range(4):
            for tg in range(4):
                k = it % 2
                it += 1
                pz, pzb = self.proj_fm(zslot, zsb, c * 128, tg)
                self.act(sz[k], pz[:, :], AF.Silu, [pzb], [szb[k]])
                self.tt("dve", self.yT[2][:, c, tg * 512:(tg + 1) * 512], self.yT[2][:, c, tg * 512:(tg + 1) * 512], sz[k], ALU.mult,
                        [self.yTb[2][c][tg], szb[k]], [self.yTb[2][c][tg]])
        if self.dbg == 12 and l == 0 and self.branches[-1] == "c":
            for h in range(4):
                self.dump(h, self.yT[2][:, h, :], self.yTb[2][h])

    def branch_m(self, l):
        P = self.P
        P.barrier()
        ar = Arena(self.arena_t, self.ARN)
        qT = ar.bf16(4 * T).rearrange("p (h t) -> p h t", h=4)
        qTb = [[Buf() for _ in range(4)] for _ in range(4)]
        mkT = ar.bf16(4 * MEM).rearrange("p (h m) -> p h m", h=4)
        mkTb = Buf()
        mv = ar.bf16(2 * 512).rearrange("p (c f) -> p c f", c=2)
        mvb = Buf()
        pT = [ar.bf16(1024).rearrange("p (c t) -> p c t", c=2) for _ in range(2)]
        pTb = [Buf(), Buf()]
        rden = [ar.f32(512) for _ in range(2)]
        rdenb = [Buf(), Buf()]
        otmp = [ar.f32(512) for _ in range(2)]
        otmpb = [Buf(), Buf()]
        sz = [ar.f32(512) for _ in range(2)]
        szb = [Buf(), Buf()]
        for t in range(2):
            xi, xb = self.xin[t % 2], self.xinb[t % 2]
            self.dma("sp", xi[:, :], self.mem[t * 128:(t + 1) * 128, :], (), [xb])
            ssq = self.stat[:, 20 + t:21 + t]
            rs = self.stat[:, 52 + t:53 + t]
            self.memset("dve", ssq, 0.0, [self.statb])
            self.act(self.xsc[:, :], xi[:, :], AF.Square, [xb, self.statb], [self.xscb, self.statb], accum=ssq)
            self.rstd(rs, ssq, 1.0 / D)
            self.act(self.xsc[:, :], xi[:, :], AF.Copy, [xb, self.statb], [self.xscb], scale=rs)
            for half in range(2):
                ps, pb = self.newps()
                for j in range(4):
                    kc = half * 4 + j
                    self.tr(ps[:, j * 128:(j + 1) * 128], self.xsc[:, kc * 128:(kc + 1) * 128], self.cst(C_ID), [self.xscb, self.cstb], [pb])
                g = self.prm(l, P_GMEM + half * 4, 4)
                self.tt("dve", self.memT[:, half * 4:half * 4 + 4, t * 128:(t + 1) * 128], ps[:, :].rearrange("p (a b) -> p a b", a=4),
                        g.unsqueeze(2).to_broadcast([128, 4, 128]), ALU.mult, [pb, self.prmb], [self.memTb])
        slot, sbf = self.wload(self.w_mem[l][:, 0:512].rearrange("(kc p) c -> p kc c", p=128), 512)
        for h in range(4):
            ps, pb = self.newps()
            for kc in range(8):
                self.mm(ps[:, 0:MEM], slot[:, kc, h * 128:(h + 1) * 128], self.memT[:, kc, :], kc == 0, kc == 7, [sbf, self.memTb], [pb])
            self.cp("act", mkT[:, h, :], ps[:, 0:MEM], [pb], [mkTb])
        slot, sbf = self.wload(self.w_mem[l][:, 512:1024].rearrange("(kc p) c -> p kc c", p=128), 512)
        for mc in range(2):
            ps, pb = self.newps()
            for kc in range(8):
                self.mm(ps[:, :], self.memT[:, kc, mc * 128:(mc + 1) * 128], slot[:, kc, :], kc == 0, kc == 7, [sbf, self.memTb], [pb])
            self.cp("act", mv[:, mc, :], ps[:, :], [pb], [mvb])
        slot, sbf = self.wload(self.win(l, 4872, 512), 512)
        for h in range(4):
            for tg in range(4):
                ps, pb = self.proj_fm(slot, sbf, h * 128, tg)
                self.cp("act", qT[:, h, tg * 512:(tg + 1) * 512], ps[:, :], [pb], [qTb[h][tg]])
        zslot, zsb = self.wload(self.win(l, 5384, 512), 512)
        it = 0
        for h in range(4):
            for tg in range(4):
                k = it % 2
                it += 1
                pss = []
                for mc in range(2):
                    ps, pb = self.newps()
                    self.mm(ps[:, :], mkT[:, h, mc * 128:(mc + 1) * 128], qT[:, h, tg * 512:(tg + 1) * 512], True, True, [mkTb, qTb[h][tg]], [pb])
                    self.act(pT[k][:, mc, :], ps[:, :], AF.Exp, [pb], [pTb[k]], scale=128 ** -0.5)
                po, pob = self.newps()
                for mc in range(2):
                    self.mm(po[:, :], mv[:, mc, h * 128:(h + 1) * 128], pT[k][:, mc, :], mc == 0, mc == 1, [mvb, pTb[k]], [pob])
                pd, pdb = self.newps()
                for mc in range(2):
                    self.mm(pd[:, :], self.onesb, pT[k][:, mc, :], mc == 0, mc == 1, [self.cb16b, pTb[k]], [pdb])
                self.act(rden[k], pd[:, :], AF.Ln, [pdb], [rdenb[k]])
                self.act(rden[k], rden[k], AF.Exp, [rdenb[k]], [rdenb[k]], scale=-1.0)
                self.tt("dve", otmp[k], po[:, :], rden[k], ALU.mult, [pob, rdenb[k]], [otmpb[k]])
                pz, pzb = self.proj_fm(zslot, zsb, h * 128, tg)
                self.act(sz[k], pz[:, :], AF.Silu, [pzb], [szb[k]])
                self.tt("dve", self.yT[3][:, h, tg * 512:(tg + 1) * 512], otmp[k], sz[k], ALU.mult, [otmpb[k], szb[k]], [self.yTb[3][h][tg]])
        if self.dbg == 12 and l == 0:
            for h in range(4):
                self.dump(h, self.yT[3][:, h, :], self.yTb[3][h])

    def merge_out(self, l):
        P = self.P
        P.barrier()
        ar = Arena(self.arena_t, self.ARN)
        if self.dbg == 28 and l == 0:
            for i in range(4):
                for c in range(4):
                    self.dump(12 + i * 4 + c, self.yT[i][:, c, :], self.yTb[i][c])
        mT = ar.bf16(8 * T).rearrange("p (c t) -> p c t", c=8)
        mTb = [[Buf() for _ in range(4)] for _ in range(8)]
        sg = [ar.f32(512) for _ in range(2)]
        sgb = [Buf(), Buf()]
        acc = [ar.f32(512) for _ in range(2)]
        accb = [Buf(), Buf()]
        tmp = [ar.f32(512) for _ in range(2)]
        tmpb = [Buf(), Buf()]
        it = 0
        ia = 0
        for dc in range(8):
            i = self.ws_i
            self.ws_i = (i + 1) % len(self.ws)
            gslot, gsb = self.ws[i], self.wsb[i]
            for n in range(4):
                c0 = 5896 + n * 1024 + dc * 128
                self.dma("pool", gslot[:, :, n * 128:(n + 1) * 128], self.win(l, c0, 128), (), [gsb])
            i = self.ws_i
            self.ws_i = (i + 1) % len(self.ws)
            uslot, usb = self.ws[i][:, :, :].rearrange("p a (b c) -> p (a b) c", c=128), self.wsb[i]
            for n in range(4):
                self.dma("pool", uslot[:, n * 4:(n + 1) * 4, :], self.w_up[l][n][:, dc * 128:(dc + 1) * 128].rearrange("(kc p) d -> p kc d", p=128), (), [usb])
            for tg in range(4):
                ka = ia % 2
                ia += 1
                for n in range(4):
                    k = it % 2
                    it += 1
                    pg, pgb = self.proj_fm(gslot, gsb, n * 128, tg)
                    self.act(sg[k], pg[:, :], AF.Sigmoid, [pgb], [sgb[k]])
                    pp, ppb = self.newps()
                    for kc in range(4):
                        self.mm(pp[:, :], uslot[:, n * 4 + kc, :], self.yT[n][:, kc, tg * 512:(tg + 1) * 512], kc == 0, kc == 3,
                                [usb, self.yTb[n][kc][tg]], [ppb])
                    if n == 0:
                        self.tt("dve", acc[ka], pp[:, :], sg[k], ALU.mult, [ppb, sgb[k]], [accb[ka]])
                    else:
                        self.tt("dve", tmp[k], pp[:, :], sg[k], ALU.mult, [ppb, sgb[k]], [tmpb[k]])
                        if n < 3:
                            self.tt("dve", acc[ka], acc[ka], tmp[k], ALU.add, [accb[ka], tmpb[k]], [accb[ka]])
                        else:
                            self.tt("dve", mT[:, dc, tg * 512:(tg + 1) * 512], acc[ka], tmp[k], ALU.add, [accb[ka], tmpb[k]], [mTb[dc][tg]])
        if self.dbg and l == 0:
            for dc in range(8):
                self.dump(4 + dc, mT[:, dc, :], mTb[dc])
        P.barrier()
        ar.off = 8 * T // 2
        wo = []
        for hh in range(2):
            slot, sbf = self.wload(self.w_out[l][:, hh * 512:(hh + 1) * 512].rearrange("(kc p) c -> p kc c", p=128), 512)
            wo.append((slot, sbf))
        osb = [ar.f32(1024) for _ in range(2)]
        osbb = [Buf(), Buf()]
        xo = [ar.f32(1024) for _ in range(2)]
        xob = [Buf(), Buf()]
        src = self.x if l == 0 else self.xs
        last = (l == self.nl - 1)
        dst = self.y if last else self.xs
        for t in range(NT):
            k = t % 2
            xi, xb = self.xin[k], self.xinb[k]
            self.dma("sp", xi[:, :], src[t * 128:(t + 1) * 128, :], [self.out_b] if l > 0 else (), [xb])
            pso = []
            for hh in range(2):
                ps, pb = self.newps()
                slot, sbf = wo[hh]
                for kc in range(8):
                    self.mm(ps[:, :], mT[:, kc, t * 128:(t + 1) * 128], slot[:, kc, :], kc == 0, kc == 7, [sbf, mTb[kc][t // 4]], [pb])
                pso.append((ps, pb))
            ssq = self.stat[:, 24 + k:25 + k]
            rs = self.stat[:, 56 + k:57 + k]
            ss2 = self.stat[:, 26 + k:27 + k]
            self.memset("dve", ssq, 0.0, [self.statb])
            self.memset("dve", ss2, 0.0, [self.statb])
            self.act(osb[k][:, 0:512], pso[0][0][:, :], AF.Square, [pso[0][1], self.statb], [osbb[k], self.statb], accum=ssq)
            self.act(osb[k][:, 512:1024], pso[1][0][:, :], AF.Square, [pso[1][1], self.statb], [osbb[k], self.statb], accum=ss2)
            self.tt("dve", ssq, ssq, ss2, ALU.add, [self.statb], [self.statb])
            self.rstd(rs, ssq, 1.0 / D)
            gp = self.prmB_t[:, P_GPOST:P_GPOST + 1024]
            for hh in range(2):
                self.stt("dve", osb[k][:, hh * 512:(hh + 1) * 512], pso[hh][0][:, :], rs, gp[:, hh * 512:(hh + 1) * 512], ALU.mult, ALU.mult,
                         [pso[hh][1], self.statb, self.prmBb], [osbb[k]])
            self.tt("pool" if False else "dve", xo[k], osb[k], xi[:, :], ALU.add, [osbb[k], xb], [xob[k]])
            self.dma("sp", dst[t * 128:(t + 1) * 128, :], xo[k], [xob[k]], [self.out_b], key="outd")
            if not last:
                self.norm_T_sb(xo[k], xob[k], l + 1, t)

    def norm_T_sb(self, xi, xb, l, t):
        ssq = self.stat[:, t:t + 1]
        rs = self.stat[:, 32 + t:33 + t]
        self.memset("dve", ssq, 0.0, [self.statb])
        self.act(self.xsc[:, :], xi, AF.Square, [xb, self.statb], [self.xscb, self.statb], accum=ssq)
        self.rstd(rs, ssq, 1.0 / D)
        self.act(self.xsc[:, :], xi, AF.Copy, [xb, self.statb], [self.xscb], scale=rs)
        for half in range(2):
            ps, pb = self.newps()
            for j in range(4):
                kc = half * 4 + j
                self.tr(ps[:, j * 128:(j + 1) * 128], self.xsc[:, kc * 128:(kc + 1) * 128], self.cst(C_ID), [self.xscb, self.cstb], [pb])
            g = self.prm(l, P_GPRE + half * 4, 4)
            self.tt("dve", self.hT[:, half * 4:half * 4 + 4, t * 128:(t + 1) * 128], ps[:, :].rearrange("p (a b) -> p a b", a=4),
                    g.unsqueeze(2).to_broadcast([128, 4, 128]), ALU.mult, [pb, self.prmb], [self.hTb[t]])


def _consts():
    i = np.arange(128)[:, None]
    j = np.arange(128)[None, :]
    blocks = [(i == j), (i <= j), (i > j), np.broadcast_to(i == 127, (128, 128)), np.ones((128, 128), bool)]
    offs = []
    for k in range(1, 8):
        offs.append((i > j) & ((i >> k) == (j >> k)) & ((i >> (k - 1)) != (j >> (k - 1))))
    blocks += offs
    blocks.append(offs[0].T)
    return np.concatenate([b.astype(np.float32) for b in blocks], axis=1)


def _params(inp):
    prm = np.zeros((128, NL * PL), np.float32)
    for l in range(NL):
        o = l * PL
        prm[:, o + P_GPRE:o + P_GPRE + 8] = inp["norm_pre"][l].reshape(8, 128).T
        prm[:, o + P_GMEM:o + P_GMEM + 8] = inp["norm_mem"][l].reshape(8, 128).T
        prm[:, o + P_CW:o + P_CW + 48] = inp["conv_w"][l].reshape(4, 12, 128).transpose(2, 1, 0).reshape(128, 48)
        prm[:, o + P_DN] = inp["dn_norm"][l]
        prm[:, o + P_GMN:o + P_GMN + 4] = inp["gm_norm"][l].reshape(4, 128).T
        sk = inp["sinks"][l].reshape(4, 2)
        prm[0:64, o + P_SINK:o + P_SINK + 4] = np.broadcast_to(sk[:, 0], (64, 4))
        prm[64:128, o + P_SINK:o + P_SINK + 4] = np.broadcast_to(sk[:, 1], (64, 4))
        prm[:, o + P_ALOG:o + P_ALOG + 4] = np.broadcast_to(inp["a_log"][l], (128, 4))
        prm[:, o + P_DTB:o + P_DTB + 4] = np.broadcast_to(inp["dt_bias"][l], (128, 4))
    prmb = np.zeros((NL, 128, PLB), np.float32)
    for l in range(NL):
        prmb[l, :, P_GPOST:P_GPOST + 1024] = np.broadcast_to(inp["norm_post"][l], (128, 1024))
        prmb[l, :, P_SB:P_SB + 512] = np.broadcast_to(inp["spatial_b"][l].reshape(512), (128, 512))
    return prm, prmb


_NC_CACHE = {}


def make_in_maps(inp, cores):
    f = lambda a: np.ascontiguousarray(np.asarray(a, dtype=np.float32))
    inp = {k: f(v) for k, v in inp.items()}
    shared = {
        "w_in": inp["w_in"], "w_mem_kv": inp["w_mem_kv"], "w_up": inp["w_up"], "w_out": inp["w_out"],
        "prm": _params(inp)[0], "prmb": _params(inp)[1], "cst": _consts(),
        "spw": np.ascontiguousarray(inp["spatial_w"].transpose(0, 1, 3, 2)),
    }
    maps = []
    for b in cores:
        m = dict(shared)
        m["x"] = inp["x"][b]
        m["mem"] = inp["mem"][b]
        maps.append(m)
    return maps


def kernel(**inputs):
    key = "full"
    if key not in _NC_CACHE:
        _NC_CACHE[key] = K().build()
    nc = _NC_CACHE[key]
    maps = make_in_maps(inputs, list(range(8)))
    res = run_bass_kernel_spmd(nc, maps, core_ids=list(range(8)))
    return np.stack([np.asarray(r["y"]) for r in res.results], axis=0).astype(np.float32)
```

```python
import contextlib
import numpy as np
import concourse.bass as bass
import concourse.mybir as mybir
from concourse.bass_utils import run_bass_kernel_spmd

F32 = mybir.dt.float32
BF16 = mybir.dt.bfloat16
AF = mybir.ActivationFunctionType
ALU = mybir.AluOpType

T = 2048
D = 1024
NT = 16
NL = 2
DIN = 9992
MEM = 256
EPS = 1e-6
ENGS = ("pe", "act", "dve", "pool", "sp")

C_ID, C_TRI, C_STRICT, C_SEL, C_ONES, C_OFF1 = 0, 1, 2, 3, 4, 5
C_OFF1T = 12
NCST = 13
P_GPRE, P_GMEM, P_CW, P_DN, P_GMN, P_SINK, P_ALOG, P_DTB = 0, 8, 16, 64, 65, 69, 73, 77
PL = 81
P_GPOST, P_SB = 0, 1024
PLB = 1024 + 512


class Buf:
    __slots__ = ("name", "w", "r", "excl")

    def __init__(self, name="", excl=False):
        self.name = name
        self.w = None
        self.r = []
        self.excl = excl


class Prog:
    def __init__(self, nc, window=10 ** 9):
        self.nc = nc
        self.ins = {e: [] for e in ENGS}
        self.seen = {e: {} for e in ENGS}
        self.dma_sems = {}
        self.window = window

    def _waits_for(self, eng, toks):
        need = {}
        my_idx = len(self.ins[eng])
        for t in toks:
            if t is None:
                continue
            k, v = t
            if k == eng:
                if eng == "pe" or eng == "sp":
                    continue
                if my_idx - v > self.window:
                    continue
            if self.seen[eng].get(k, -1) >= v:
                continue
            if need.get(k, -1) < v:
                need[k] = v
        for k, v in need.items():
            self.seen[eng][k] = v
        return list(need.items())

    def _toks(self, reads, writes):
        toks = []
        for b in reads:
            toks.append(b.w)
        for b in writes:
            toks.append(b.w)
            toks.extend(b.r)
        return toks

    def op(self, eng, fn, reads=(), writes=()):
        xr = [b for b in reads if b.excl]
        if xr:
            writes = list(writes) + [b for b in xr if b not in writes]
            reads = [b for b in reads if not b.excl]
        waits = self._waits_for(eng, self._toks(reads, writes))
        idx = len(self.ins[eng])
        self.ins[eng].append([fn, waits, False, None])
        tok = (eng, idx)
        for b in reads:
            b.r.append(tok)
        for b in writes:
            b.w = tok
            b.r = []
        return tok

    def dma(self, eng, fn, reads=(), writes=(), key=None):
        waits = self._waits_for(eng, self._toks(reads, writes))
        if key is None:
            key = id(writes[0] if writes else reads[0])
        ent = self.dma_sems.setdefault(key, [None, 0])
        ent[1] += 16
        tok = (("dma", key), ent[1])
        self.ins[eng].append([fn, waits, False, key])
        for b in reads:
            b.r.append(tok)
        for b in writes:
            b.w = tok
            b.r = []
        return tok

    def barrier(self):
        last = {}
        for e in ("pe", "act", "dve", "pool"):
            n = len(self.ins[e])
            j = n - 1
            while j >= 0 and (self.ins[e][j][3] is not None or self.ins[e][j][0] is None):
                j -= 1
            if j >= 0:
                last[e] = j
        toks = [(e, j) for e, j in last.items()]
        for key, ent in self.dma_sems.items():
            toks.append((("dma", key), ent[1]))
        for e in ("pe", "act", "dve", "pool", "sp"):
            waits = []
            for k, v in toks:
                if k == e and e in ("pe", "sp"):
                    continue
                if self.seen[e].get(k, -1) >= v:
                    continue
                self.seen[e][k] = v
                waits.append((k, v))
            if waits:
                self.ins[e].append([None, waits, False, None])

    def emit(self, final_toks=()):
        nc = self.nc
        for e in ENGS:
            for ent in self.ins[e]:
                for k, v in ent[1]:
                    if isinstance(k, str):
                        self.ins[k][v][2] = True
        for k, v in final_toks:
            if isinstance(k, str):
                self.ins[k][v][2] = True
        cnt = {}
        for e in ENGS:
            c = 0
            arr = []
            for ent in self.ins[e]:
                if ent[2]:
                    c += 1
                arr.append(c)
            cnt[e] = arr
        with contextlib.ExitStack() as st:
            esem = {e: st.enter_context(nc.semaphore("s_" + e)) for e in ENGS}
            for i, key in enumerate(self.dma_sems):
                self.dma_sems[key][0] = st.enter_context(nc.semaphore("d%d" % i))
            block = st.enter_context(nc.Block())

            def semval(k, v):
                if isinstance(k, str):
                    return esem[k], cnt[k][v]
                return self.dma_sems[k[1]][0], v

            def body(e, eng):
                for fn, waits, sig, dkey in self.ins[e]:
                    for k, v in waits:
                        s, val = semval(k, v)
                        eng.wait_ge(s, val)
                    if fn is None:
                        continue
                    inst = fn(eng)
                    if dkey is not None:
                        inst.then_inc(self.dma_sems[dkey][0], 16)
                    elif sig:
                        inst.then_inc(esem[e], 1)
                if e == "sp":
                    for k, v in final_toks:
                        s, val = semval(k, v)
                        eng.wait_ge(s, val)

            @block.tensor
            def _(eng):
                body("pe", eng)

            @block.scalar
            def _(eng):
                body("act", eng)

            @block.vector
            def _(eng):
                body("dve", eng)

            @block.gpsimd
            def _(eng):
                body("pool", eng)

            @block.sync
            def _(eng):
                body("sp", eng)


class Arena:
    def __init__(self, ap, n):
        self.ap = ap
        self.n = n
        self.off = 0

    def f32(self, n):
        assert self.off + n <= self.n, ("arena overflow", self.off, n, self.n)
        a = self.ap[:, self.off:self.off + n]
        self.off += n
        return a

    def bf16(self, n):
        return self.f32((n + 1) // 2).bitcast(BF16)[:, 0:n]


class K:
    def __init__(self, nlayers=NL, branches="abcm", dbg=None):
        self.nl = nlayers
        self.branches = branches
        self.dbg = dbg
        self.qshare = True
        nc = self.nc = bass.Bass("TRN2", target_bir_lowering=False)
        dr = lambda n, s, k="ExternalInput": nc.dram_tensor(n, s, F32, kind=k).ap()
        self.x = dr("x", [T, D])
        self.mem = dr("mem", [MEM, D])
        self.w_in = dr("w_in", [NL, D, DIN])
        self.w_mem = dr("w_mem_kv", [NL, D, 1024])
        self.w_up = dr("w_up", [NL, 4, 512, D])
        self.w_out = dr("w_out", [NL, D, D])
        self.prm_d = dr("prm", [128, NL * PL])
        self.prmb_d = dr("prmb", [NL, 128, PLB])
        self.cst_d = dr("cst", [128, NCST * 128])
        self.spw_d = dr("spw", [NL, 4, 128, 128])
        self.y = dr("y", [T, D], "ExternalOutput")
        self.xs = dr("xs", [T, D], "Internal")
        if dbg:
            self.dbg_d = dr("dbg", [dbg, 128, 2048], "ExternalOutput")

    def mm(self, out, lhsT, rhs, start, stop, reads, writes):
        self.P.op("pe", lambda e: e.matmul(out, lhsT=lhsT, rhs=rhs, start=start, stop=stop), reads, writes)

    def tr(self, out, in_, ident, reads, writes):
        self.P.op("pe", lambda e: e.transpose(out=out, in_=in_, identity=ident), reads, writes)

    def act(self, out, in_, func, reads, writes, bias=0.0, scale=1.0, accum=None):
        if accum is None:
            self.P.op("act", lambda e: e.activation(out=out, in_=in_, func=func, bias=bias, scale=scale), reads, writes)
        else:
            self.P.op("act", lambda e: e.activation(out=out, in_=in_, func=func, bias=bias, scale=scale, accum_out=accum), reads, writes)

    def tt(self, eng, out, in0, in1, op, reads, writes):
        self.P.op(eng, lambda e: e.tensor_tensor(out=out, in0=in0, in1=in1, op=op), reads, writes)

    def ts(self, eng, out, in0, s1, s2, op0, op1, reads, writes):
        if s2 is None:
            self.P.op(eng, lambda e: e.tensor_scalar(out=out, in0=in0, scalar1=s1, scalar2=None, op0=op0), reads, writes)
        else:
            self.P.op(eng, lambda e: e.tensor_scalar(out=out, in0=in0, scalar1=s1, scalar2=s2, op0=op0, op1=op1), reads, writes)

    def stt(self, eng, out, in0, scalar, in1, op0, op1, reads, writes):
        self.P.op(eng, lambda e: e.scalar_tensor_tensor(out=out, in0=in0, scalar=scalar, in1=in1, op0=op0, op1=op1), reads, writes)

    def cp(self, eng, out, in_, reads, writes):
        if eng == "act":
            self.P.op("act", lambda e: e.copy(out=out, in_=in_), reads, writes)
        else:
            self.P.op(eng, lambda e: e.tensor_copy(out=out, in_=in_), reads, writes)

    def recip(self, out, in_, reads, writes):
        self.P.op("dve", lambda e: e.reciprocal(out=out, in_=in_), reads, writes)

    def memset(self, eng, ap, val, writes):
        self.P.op(eng, lambda e: e.memset(ap, val), (), writes)

    def dma(self, eng, out, in_, reads, writes, key=None):
        self.P.dma(eng, lambda e: e.dma_start(out=out, in_=in_), reads, writes, key)

    def newps(self):
        i = self.ps_i
        self.ps_i = (i + 1) % 8
        return self.ps[i], self.psb[i]

    def cst(self, i, n=128):
        return self.cst_t[:, i * 128:i * 128 + n]

    def prm(self, l, off, n):
        return self.prm_t[:, l * PL + off:l * PL + off + n]

    def wload(self, src3, ncols, kc=8):
        i = self.ws_i
        self.ws_i = (i + 1) % len(self.ws)
        slot, b = self.ws[i], self.wsb[i]
        self.dma("pool", slot[:, 0:kc, 0:ncols], src3, (), [b])
        return slot, b

    def win(self, l, c0, n):
        return self.w_in[l][:, c0:c0 + n].rearrange("(kc p) c -> p kc c", p=128)

    def proj_fm(self, slot, sb, co, tg, extra_reads=()):
        ps, pb = self.newps()
        for kc in range(8):
            self.mm(ps[:, :], slot[:, kc, co:co + 128], self.hT[:, kc, tg * 512:(tg + 1) * 512], kc == 0, kc == 7,
                    [sb] + self.hTb[tg * 4:(tg + 1) * 4], [pb])
        return ps, pb

    def dump(self, idx, ap, reads, dt_is_bf16=False, n=2048, part=128):
        if not self.dbg:
            return
        st = self.dbg_stage
        b = self.dbg_stage_b
        self.cp("dve", st[0:part, 0:n], ap, reads, [b])
        self.dma("sp", self.dbg_d[idx][0:part, 0:n], st[0:part, 0:n], [b], [self.dbg_out_b], key="dbg")

    def build(self):
        nc = self.nc
        with contextlib.ExitStack() as st:
            self.st = st
            sb = lambda n, s, d=F32: st.enter_context(nc.sbuf_tensor("sb_" + n, s, d))
            self.P = Prog(nc)
            self.ps = [st.enter_context(nc.psum_tensor("ps%d" % i, [128, 512], F32)) for i in range(8)]
            self.psb = [Buf("ps%d" % i, excl=True) for i in range(8)]
            self.ps_i = 0
            self.cst_t = sb("cst", [128, NCST * 128])
            self.prm_t = sb("prm", [128, NL * PL])
            self.cstb = Buf("cst")
            self.prmb = Buf("prm")
            self.prmB_t = sb("prmB", [128, PLB])
            self.prmBb = Buf("prmB")
            self.hT = sb("hT", [128, 8, T], BF16)
            self.hTb = [Buf("hT%d" % i) for i in range(NT)]
            self.ws = [sb("ws%d" % i, [128, 8, 512], BF16) for i in range(3 if self.dbg else 4)]
            self.wsb = [Buf("ws%d" % i) for i in range(len(self.ws))]
            self.ws_i = 0
            nslots = 3 if self.dbg else 4
            self.TAIL = 12300
            self.big = sb("big", [128, 12288 + self.TAIL])
            self.yT = [sb("yT0", [128, 4, T], BF16)] + [
                self.big[:, i * 4096:(i + 1) * 4096].bitcast(BF16).rearrange("p (c t) -> p c t", c=4) for i in range(3)]
            self.yTb = [[[Buf() for tg in range(4)] for c in range(4)] for i in range(4)]
            self.cb16 = sb("cb16", [128, 5 * 128], BF16)
            self.cb16b = Buf("cb16")
            self.xin = [sb("xin%d" % i, [128, D]) for i in range(2)]
            self.xinb = [Buf() for i in range(2)]
            self.xsc = sb("xsc", [128, D])
            self.xscb = Buf()
            self.stat = sb("stat", [128, 64])
            self.statb = Buf()
            self.memT = sb("memT", [128, 8, MEM], BF16)
            self.memTb = Buf()
            self.arena_t = self.big[:, 12288:12288 + self.TAIL]
            self.ARN = self.TAIL
            if self.dbg:
                self.dbg_stage = sb("dbgst", [128, 2048])
                self.dbg_stage_b = Buf()
                self.dbg_out_b = Buf()
            self.out_b = Buf("out")

            self.dma("sp", self.cst_t[:, :], self.cst_d[:, :], (), [self.cstb])
            self.dma("sp", self.prm_t[:, :], self.prm_d[:, :], (), [self.prmb])
            self.cp("act", self.cb16[:, 0:128], self.cst(C_ID), [self.cstb], [self.cb16b])
            self.cp("act", self.cb16[:, 128:256], self.cst(C_TRI), [self.cstb], [self.cb16b])
            self.cp("act", self.cb16[:, 256:384], self.cst(C_ONES), [self.cstb], [self.cb16b])
            self.memset("dve", self.cb16[:, 384:576], 0.0, [self.cb16b])
            self.cp("act", self.cb16[:, 448:512], self.cst(C_ONES, 64), [self.cstb], [self.cb16b])
            self.identb = self.cb16[:, 0:128]
            self.trib = self.cb16[:, 128:256]
            self.onesb = self.cb16[:, 256:384]
            self.onesO = self.cb16[:, 384:512]
            self.onesE = self.cb16[:, 448:576]

            for t in range(NT):
                xi, xb = self.xin[t % 2], self.xinb[t % 2]
                self.dma("sp", xi[:, :], self.x[t * 128:(t + 1) * 128, :], (), [xb])
                self.norm_T(xi, xb, 0, t)
            for l in range(self.nl):
                self.layer(l)
            self.P.emit(final_toks=[self.out_b.w])
        return nc

    def norm_T(self, xi, xb, l, t):
        ssq = self.stat[:, t:t + 1]
        rs = self.stat[:, 32 + t:33 + t]
        self.memset("dve", ssq, 0.0, [self.statb])
        self.act(self.xsc[:, :], xi[:, :], AF.Square, [xb, self.statb], [self.xscb, self.statb], accum=ssq)
        self.rstd(rs, ssq, 1.0 / D)
        self.act(self.xsc[:, :], xi[:, :], AF.Copy, [xb, self.statb], [self.xscb], scale=rs)
        for half in range(2):
            ps, pb = self.newps()
            for j in range(4):
                kc = half * 4 + j
                self.tr(ps[:, j * 128:(j + 1) * 128], self.xsc[:, kc * 128:(kc + 1) * 128], self.cst(C_ID), [self.xscb, self.cstb], [pb])
            g = self.prm(l, P_GPRE + half * 4, 4)
            self.tt("dve", self.hT[:, half * 4:half * 4 + 4, t * 128:(t + 1) * 128], ps[:, :].rearrange("p (a b) -> p a b", a=4),
                    g.unsqueeze(2).to_broadcast([128, 4, 128]), ALU.mult, [pb, self.prmb], [self.hTb[t]])

    def rstd(self, out, ssq, scale, eps=EPS):
        self.act(out, ssq, AF.Ln, [self.statb], [self.statb], bias=eps, scale=scale)
        self.act(out, out, AF.Exp, [self.statb], [self.statb], scale=-0.5)

    def layer(self, l):
        P = self.P
        self.dma("sp", self.prmB_t[:, :], self.prmb_d[l], (), [self.prmBb])
        if "a" in self.branches:
            self.branch_a(l)
        if "b" in self.branches:
            self.branch_b(l)
        if "c" in self.branches:
            self.branch_c(l)
        if "m" in self.branches:
            self.branch_m(l)
        P.barrier()
        for i in range(4):
            if "abcm"[i] not in self.branches:
                for c in range(4):
                    self.memset("dve", self.yT[i][:, c, :], 0.0, [b for b in self.yTb[i][c]])
        self.merge_out(l)


    def branch_a(self, l):
        P = self.P
        P.barrier()
        ar = Arena(self.big[:, 0:12288 + self.TAIL], 12288 + self.TAIL)
        qT = ar.bf16(4 * T).rearrange("p (c t) -> p c t", c=4)
        kT = ar.bf16(4 * T).rearrange("p (c t) -> p c t", c=4)
        qkb = {("q", h): [Buf() for _ in range(4)] for h in range(4)}
        qkb.update({("k", h): [Buf() for _ in range(4)] for h in range(4)})
        ktok = ar.bf16(16 * 512).rearrange("p (t h d) -> p t h d", t=16, h=4)
        vtok = ar.bf16(16 * 512).rearrange("p (t h d) -> p t h d", t=16, h=4)
        ktokb = [[Buf() for _ in range(2)] for _ in range(4)]
        vtokb = [[Buf() for _ in range(2)] for _ in range(4)]
        gt = {n_: ar.f32(64) for n_ in ("g", "beta", "gc", "ngc", "egc", "begc", "ekd", "glb", "dl", "tmp")}
        ea = ar.f32(4)
        ghl = ar.bf16(128)
        ghi, glo = ghl[:, 0:64], ghl[:, 64:128]
        gb = Buf("gates")
        mark = ar.off
        raw = ar.f32(2052)
        rawb = Buf()
        acc = ar.f32(T)
        accb = [Buf(), Buf()]
        sq = ar.bf16(T)
        sqb = Buf()
        vT = ar.bf16(T)
        vTb = Buf()
        rn = [ar.f32(512) for _ in range(2)]
        rnb = [Buf(), Buf()]
        self.memset("dve", raw[:, 0:4], 0.0, [rawb])
        slot = sbf = None
        for ci in range(12):
            kind, h = "qkv"[ci // 4], ci % 4
            if ci % 4 == 0:
                slot, sbf = self.wload(self.win(l, ci * 128, 512), 512)
            for tg in range(4):
                ps, pb = self.proj_fm(slot, sbf, (ci % 4) * 128, tg)
                self.cp("act", raw[:, 3 + tg * 512:3 + (tg + 1) * 512], ps[:, :], [pb], [rawb])
            for hf, eng in ((0, "dve"), (1, "dve")):
                t0 = hf * 1024
                cw = lambda k_: self.prm(l, P_CW + ci * 4 + k_, 1)
                self.ts(eng, acc[:, t0:t0 + 1024], raw[:, 3 + t0:3 + t0 + 1024], cw(3), None, ALU.mult, None, [rawb, self.prmb], [accb[hf]])
                for k_ in (2, 1, 0):
                    self.stt(eng, acc[:, t0:t0 + 1024], raw[:, k_ + t0:k_ + t0 + 1024], cw(k_), acc[:, t0:t0 + 1024], ALU.mult, ALU.add,
                             [rawb, self.prmb, accb[hf]], [accb[hf]])
            if kind == "v":
                self.act(vT[:, :], acc[:, :], AF.Silu, accb, [vTb])
                src, srcb, dst, dstb = vT, [vTb], vtok, vtokb[h]
            else:
                self.act(acc[:, :], acc[:, :], AF.Silu, accb, accb)
                self.act(sq[:, :], acc[:, :], AF.Square, accb, [sqb])
                dT = qT if kind == "q" else kT
                sc = 128.0 if kind == "q" else 1.0
                for tg in range(4):
                    k = tg % 2
                    ps, pb = self.newps()
                    self.mm(ps[:, :], self.onesb, sq[:, tg * 512:(tg + 1) * 512], True, True, [self.cb16b, sqb], [pb])
                    self.act(rn[k], ps[:, :], AF.Ln, [pb], [rnb[k]], bias=EPS * sc, scale=sc)
                    self.act(rn[k], rn[k], AF.Exp, [rnb[k]], [rnb[k]], scale=-0.5)
                    self.tt("dve", dT[:, h, tg * 512:(tg + 1) * 512], acc[:, tg * 512:(tg + 1) * 512], rn[k], ALU.mult, [accb[tg // 2], rnb[k]], [qkb[(kind, h)][tg]])
                if kind == "k":
                    src, srcb, dst, dstb = kT[:, h, :], None, ktok, ktokb[h]
            if kind in "kv":
                for half in range(2):
                    ps, pb = self.newps()
                    psv = ps[:, :].bitcast(BF16)
                    for j in range(8):
                        t = half * 8 + j
                        rd = [vTb] if kind == "v" else [qkb[("k", h)][t // 4]]
                        self.tr(psv[:, j * 128:(j + 1) * 128], src[:, t * 128:(t + 1) * 128], self.identb, rd + [self.cb16b], [pb])
                    self.cp("act", dst[:, half * 8:half * 8 + 8, h, :], psv[:, :].rearrange("p (a b) -> p a b", a=8), [pb], [dstb[half]])
        gslot, gsb = self.wload(self.win(l, 2048, 8), 8)
        psg, psgb = self.newps()
        for t in range(NT):
            for kc in range(8):
                self.mm(psg[:, t * 8:(t + 1) * 8], self.hT[:, kc, t * 128:(t + 1) * 128], gslot[:, kc, 0:8], kc == 0, kc == 7, [gsb, self.hTb[t]], [psgb])
        pv = psg[:, 0:128].rearrange("p (t j) -> p t j", t=16)
        v3 = lambda a: a.rearrange("p (t j) -> p t j", t=16)
        self.act(v3(gt["beta"]), pv[:, :, 0:4], AF.Sigmoid, [psgb], [gb])
        self.tt("dve", v3(gt["tmp"]), pv[:, :, 4:8], self.prm(l, P_DTB, 4).unsqueeze(1).to_broadcast([128, 16, 4]), ALU.add, [psgb, self.prmb], [gb])
        self.act(gt["tmp"], gt["tmp"], AF.Exp, [gb], [gb])
        self.act(gt["tmp"], gt["tmp"], AF.Ln, [gb], [gb], bias=1.0)
        self.act(ea, self.prm(l, P_ALOG, 4), AF.Exp, [self.prmb], [gb])
        self.stt("dve", v3(gt["g"]), v3(gt["tmp"]), -1.0, ea.unsqueeze(1).to_broadcast([128, 16, 4]), ALU.mult, ALU.mult, [gb], [gb])
        ps, pb = self.newps()
        self.mm(ps[:, 0:64], self.cst(C_TRI), gt["g"], True, True, [self.cstb, gb], [pb])
        self.cp("dve", gt["gc"], ps[:, 0:64], [pb], [gb])
        ps, pb = self.newps()
        self.mm(ps[:, 0:64], self.cst(C_SEL), gt["gc"], True, True, [self.cstb, gb], [pb])
        self.cp("dve", gt["glb"], ps[:, 0:64], [pb], [gb])
        self.act(gt["egc"], gt["gc"], AF.Exp, [gb], [gb])
        self.act(gt["dl"], gt["glb"], AF.Exp, [gb], [gb])
        self.ts("dve", gt["ngc"], gt["gc"], -1.0, None, ALU.mult, None, [gb], [gb])
        self.tt("dve", gt["ekd"], gt["glb"], gt["gc"], ALU.subtract, [gb], [gb])
        self.act(gt["ekd"], gt["ekd"], AF.Exp, [gb], [gb])
        self.tt("dve", gt["begc"], gt["beta"], gt["egc"], ALU.mult, [gb], [gb])
        self.cp("dve", ghi, gt["g"], [gb], [gb])
        self.cp("dve", gt["tmp"], ghi, [gb], [gb])
        self.tt("dve", gt["tmp"], gt["g"], gt["tmp"], ALU.subtract, [gb], [gb])
        self.cp("dve", glo, gt["tmp"], [gb], [gb])
        P.barrier()
        ar.off = mark
        extra = [Arena(self.xin[0][:, :], 1024), Arena(self.xin[1][:, :], 1024), Arena(self.xsc[:, :], 1024),
                 Arena(self.memT[:, :, :].rearrange("p a b -> p (a b)").bitcast(F32), 1024)]

        def alloc(n_f32):
            for a_ in [ar] + extra:
                if a_.off + n_f32 <= a_.n:
                    return a_.f32(n_f32)
            raise AssertionError("A3 scratch overflow")

        def mkset(fnames, bnames):
            d = {}
            for nm in fnames:
                d[nm] = alloc(128)
            for nm in bnames:
                d[nm] = alloc(64).bitcast(BF16)
            d["b"] = {nm: Buf() for nm in d.keys()}
            return d
        Sf = ar.f32(512).rearrange("p (h d) -> p h d", h=4)
        Sb = ar.bf16(512).rearrange("p (h d) -> p h d", h=4)
        Sfb = [Buf() for _ in range(4)]
        Sbb = [Buf() for _ in range(4)]
        NSC, NOP = 4, 8
        scr = [mkset(("D1", "D2", "L", "AT", "X0", "X1", "XT0", "XT1"), ()) for _ in range(NSC)]
        for d in scr:
            d["Y"] = d["L"]
            d["b"]["Y"] = d["b"]["L"]
        ops = [mkset((), ("XTb", "nwT", "kbg", "vb", "kd", "aqkT", "vnb")) for _ in range(NOP)]
        scb = []
        for _ in range(4):
            d = mkset(("t1", "o"), ("onb",))
            d["st"] = alloc(4)
            d["b"]["st"] = Buf()
            scb.append(d)
        for h in range(4):
            self.memset("dve", Sf[:, h, :], 0.0, [Sfb[h]])
            self.memset("dve", Sb[:, h, :], 0.0, [Sbb[h]])
        ID, TRI, STR = self.cst(C_ID), self.cst(C_TRI), self.cst(C_STRICT)
        qbank = [Buf() for _ in range(8)]
        qps = [(self.ps[i // 4][:, (i % 4) * 128:(i % 4 + 1) * 128], qbank[i // 4] if self.qshare else Buf()) for i in range(32)]
        qi = [0]

        def newq():
            bk_, bB_ = self.newps()
            return (bk_[:, 0:128], bB_)

        def inv_gen(n, h, d, o):
            B, OB = d["b"], o["b"]
            ck = slice(n * 128, (n + 1) * 128)
            j = n * 4 + h
            col = lambda nm: gt[nm][:, j:j + 1]
            kb_, qb_ = qkb[("k", h)][n // 4], qkb[("q", h)][n // 4]
            bk_, bB_ = self.ps[h], self.psb[h]
            newq = lambda: (bk_[:, 384:512], bB_)
            (KK, bKK), (QKT, bQK), (GCR, bG) = (bk_[:, 0:128], bB_), (bk_[:, 128:256], bB_), (bk_[:, 256:384], bB_)
            self.mm(KK, kT[:, h, ck], kT[:, h, ck], True, True, [kb_], [bKK])
            self.mm(QKT, kT[:, h, ck], qT[:, h, ck], True, True, [kb_, qb_], [bQK])
            self.mm(GCR, ghi[:, j:j + 1].to_broadcast([128, 128]), self.trib, True, False, [gb, self.cb16b], [bG])
            self.mm(GCR, glo[:, j:j + 1].to_broadcast([128, 128]), self.trib, False, True, [gb, self.cb16b], [bG])
            yield
            self.act(d["D1"], GCR, AF.Exp, [bG, gb], [B["D1"]], bias=col("gc"), scale=-1.0)
            self.act(d["D2"], GCR, AF.Exp, [bG, gb], [B["D2"]], bias=col("ngc"), scale=1.0)
            yield
            self.stt("dve", d["D1"], d["D1"], 1.0, STR, ALU.min, ALU.mult, [B["D1"], self.cstb], [B["D1"]])
            self.stt("dve", d["D2"], d["D2"], 1.0, TRI, ALU.min, ALU.mult, [B["D2"], self.cstb], [B["D2"]])
            yield
            self.stt("dve", d["L"], KK, col("beta"), d["D1"], ALU.mult, ALU.mult, [bKK, gb, B["D1"]], [B["L"]])
            self.tt("dve", o["aqkT"], QKT, d["D2"], ALU.mult, [bQK, B["D2"]], [OB["aqkT"]])
            yield
            (pt_, bpt) = newq()
            self.tr(pt_, d["L"], ID, [B["L"], self.cstb], [bpt])
            self.tt("pool", d["X0"], d["L"], self.cst(C_OFF1), ALU.mult, [B["L"], self.cstb], [B["X0"]])
            yield
            self.tt("dve", d["AT"], pt_, ID, ALU.add, [bpt, self.cstb], [B["AT"]])
            self.tt("pool", d["X0"], ID, d["X0"], ALU.subtract, [B["X0"], self.cstb], [B["X0"]])
            yield
            self.tt("pool", d["XT0"], d["AT"], self.cst(C_OFF1T), ALU.mult, [B["AT"], self.cstb], [B["XT0"]])
            yield
            self.tt("pool", d["XT0"], ID, d["XT0"], ALU.subtract, [B["XT0"], self.cstb], [B["XT0"]])
            yield
            cur = 0
            for lv in range(2, 8):
                X, XT, Xn, XTn = d["X%d" % cur], d["XT%d" % cur], d["X%d" % (1 - cur)], d["XT%d" % (1 - cur)]
                bX, bXT, bXn, bXTn = B["X%d" % cur], B["XT%d" % cur], B["X%d" % (1 - cur)], B["XT%d" % (1 - cur)]
                (pY, bpY) = (bk_[:, 0:128], bB_)
                self.mm(pY, d["AT"], X, True, True, [B["AT"], bX], [bpY])
                yield
                self.tt("dve", d["Y"], pY, self.cst(C_OFF1 + lv - 1), ALU.mult, [bpY, self.cstb], [B["Y"]])
                yield
                if lv < 7:
                    (pZ, bpZ) = (bk_[:, 128:256], bB_)
                    self.mm(pZ, XT, d["Y"], True, True, [bXT, B["Y"]], [bpZ])
                (pZT, bpZT) = (bk_[:, 256:384], bB_)
                self.mm(pZT, d["Y"], XT, True, True, [bXT, B["Y"]], [bpZT])
                yield
                if lv < 7:
                    self.tt("dve", Xn, X, pZ, ALU.subtract, [bpZ, bX], [bXn])
                self.tt("dve", XTn, XT, pZT, ALU.subtract, [bpZT, bXT], [bXTn])
                yield
                cur = 1 - cur
            XTf, bXTf = d["XT%d" % cur], B["XT%d" % cur]
            self.cp("act", o["XTb"], XTf, [bXTf], [OB["XTb"]])
            self.act(o["kbg"], ktok[:, n, h, :], AF.Copy, [ktokb[h][n // 8], gb], [OB["kbg"]], scale=col("begc"))
            self.act(o["vb"], vtok[:, n, h, :], AF.Copy, [vtokb[h][n // 8], gb], [OB["vb"]], scale=col("beta"))
            self.act(o["kd"], ktok[:, n, h, :], AF.Copy, [ktokb[h][n // 8], gb], [OB["kd"]], scale=col("ekd"))
            yield
            (pw, bpw) = newq()
            self.mm(pw, o["kbg"], o["XTb"], True, True, [OB["kbg"], OB["XTb"]], [bpw])
            yield
            self.act(o["nwT"], pw, AF.Copy, [bpw], [OB["nwT"]], scale=-1.0)
            yield

        def scan_gen(n, h, o, d):
            B, OB = d["b"], o["b"]
            ck = slice(n * 128, (n + 1) * 128)
            j = n * 4 + h
            col = lambda nm: gt[nm][:, j:j + 1]
            qb_ = qkb[("q", h)][n // 4]
            bk_, bB_ = self.ps[4 + h], self.psb[4 + h]
            newq = lambda: (bk_[:, 0:128], bB_)
            (VN, bVN), (P1, bP1) = (bk_[:, 0:128], bB_), (bk_[:, 128:256], bB_)
            self.mm(VN, o["XTb"], o["vb"], True, False, [OB["XTb"], OB["vb"]], [bVN])
            self.mm(VN, o["nwT"], Sb[:, h, :], False, True, [OB["nwT"], Sbb[h]], [bVN])
            self.mm(P1, qT[:, h, ck], Sb[:, h, :], True, True, [qb_, Sbb[h]], [bP1])
            yield
            self.cp("act", o["vnb"], VN, [bVN], [OB["vnb"]])
            yield
            (P2, bP2), (SU, bSU) = (bk_[:, 256:384], bB_), (bk_[:, 384:512], bB_)
            self.mm(SU, o["kd"], o["vnb"], True, True, [OB["kd"], OB["vnb"]], [bSU])
            self.mm(P2, o["aqkT"], o["vnb"], True, True, [OB["aqkT"], OB["vnb"]], [bP2])
            self.act(d["t1"], P1, AF.Copy, [bP1, gb], [B["t1"]], scale=col("egc"))
            yield
            self.stt("dve", Sf[:, h, :], Sf[:, h, :], col("dl"), SU, ALU.mult, ALU.add, [Sfb[h], gb, bSU], [Sfb[h]])
            yield
            self.cp("act", Sb[:, h, :], Sf[:, h, :], [Sfb[h]], [Sbb[h]])
            self.tt("dve", d["o"], d["t1"], P2, ALU.add, [B["t1"], bP2], [B["o"]])
            ssq, rs = d["st"][:, 0:1], d["st"][:, 1:2]
            self.memset("dve", ssq, 0.0, [B["st"]])
            yield
            self.act(d["t1"], d["o"], AF.Square, [B["o"], B["st"]], [B["t1"], B["st"]], accum=ssq)
            yield
            self.act(rs, ssq, AF.Ln, [B["st"]], [B["st"]], bias=EPS, scale=1.0 / 128)
            yield
            self.act(rs, rs, AF.Exp, [B["st"]], [B["st"]], scale=-0.5)
            yield
            self.act(d["onb"], d["o"], AF.Copy, [B["o"], B["st"]], [B["onb"]], scale=rs)
            yield
            (pt_, bpt) = newq()
            ptv = pt_.bitcast(BF16)
            self.tr(ptv[:, 0:128], d["onb"], self.identb, [B["onb"], self.cb16b], [bpt])
            yield
            self.ts("dve", self.yT[0][:, h, ck], ptv[:, 0:128], self.prm(l, P_DN, 1), None, ALU.mult, None, [bpt, self.prmb], [self.yTb[0][h][n // 4]])
            yield

        import os
        GW = int(os.environ.get("A3_G", "8"))

        def run(gens_all):
            gens_all = list(gens_all)
            for g0 in range(0, len(gens_all), GW):
                run1(gens_all[g0:g0 + GW])

        def run1(gens):
            gens = list(gens)
            while gens:
                nxt = []
                for g_ in gens:
                    try:
                        next(g_)
                        nxt.append(g_)
                    except StopIteration:
                        pass
                gens = nxt

        NMAX = int(os.environ.get("A3_NMAX", str(NT)))
        NOSCAN = int(os.environ.get("A3_NOSCAN", "0"))
        run([inv_gen(0, h, scr[h], ops[h]) for h in range(4)])
        for n in range(NMAX):
            gens = [scan_gen(n, h, ops[(n % 2) * 4 + h], scb[h]) for h in range(4)] if not NOSCAN else []
            if n + 1 < NMAX:
                gens += [inv_gen(n + 1, h, scr[h], ops[((n + 1) % 2) * 4 + h]) for h in range(4)]
            run(gens)
        P.barrier()
        ar.off = mark
        szs = [ar.f32(512) for _ in range(2)]
        szb = [Buf(), Buf()]
        zslot, zsb = self.wload(self.win(l, 1536, 512), 512)
        it = 0
        for c in range(4):
            for tg in range(4):
                k = it % 2
                it += 1
                pz, pzb = self.proj_fm(zslot, zsb, c * 128, tg)
                self.act(szs[k], pz[:, :], AF.Silu, [pzb], [szb[k]])
                self.tt("dve", self.yT[0][:, c, tg * 512:(tg + 1) * 512], self.yT[0][:, c, tg * 512:(tg + 1) * 512], szs[k], ALU.mult,
                        [self.yTb[0][c][tg], szb[k]], [self.yTb[0][c][tg]])
        if self.dbg == 12 and l == 0 and self.branches[-1] == "a":
            for h in range(4):
                self.dump(h, self.yT[0][:, h, :], self.yTb[0][h])

    def branch_b(self, l):
        P = self.P
        P.barrier()
        ar = Arena(self.big[:, 4096:12288 + self.TAIL], 8192 + self.TAIL)
        vtok = ar.bf16(16 * 512).rearrange("p (t c) -> p t c", t=16)
        vtokb = [Buf() for _ in range(16)]
        uT = ar.bf16(4 * T).rearrange("p (c t) -> p c t", c=4)
        uTb = [[Buf() for _ in range(4)] for _ in range(4)]
        wsT = ar.bf16(512).rearrange("p (g q) -> p g q", g=4)
        wsTb = Buf()
        wtmp = ar.f32(512).rearrange("p (g q) -> p g q", g=4)
        wtmpb = Buf()
        gv = [ar.f32(512) for _ in range(2)]
        gvb = [Buf(), Buf()]
        junk = ar.f32(512)
        junkb = Buf()
        ssb = [ar.f32(512) for _ in range(2)]
        ssbb = [Buf(), Buf()]
        sz = [ar.f32(512) for _ in range(2)]
        szb = [Buf(), Buf()]
        self.dma("sp", wtmp[:, :, :], self.spw_d[l].rearrange("g q p -> q g p"), (), [wtmpb])
        self.tt("dve", wsT[:, :, :], wtmp[:, :, :], self.cst(C_TRI).unsqueeze(1).to_broadcast([128, 4, 128]), ALU.mult, [wtmpb, self.cstb], [wsTb])
        slot, sbf = self.wload(self.win(l, 2568, 512), 512)
        for t in range(NT):
            k = t % 2
            ps, pb = self.newps()
            for kc in range(8):
                self.mm(ps[:, :], self.hT[:, kc, t * 128:(t + 1) * 128], slot[:, kc, :], kc == 0, kc == 7, [sbf, self.hTb[t]], [pb])
            self.act(gv[k], ps[:, :], AF.Gelu_apprx_tanh, [pb], [gvb[k]])
            ssq = self.stat[:, t:t + 1]
            rs = self.stat[:, 32 + t:33 + t]
            self.memset("dve", ssq, 0.0, [self.statb])
            self.act(junk, gv[k], AF.Square, [gvb[k], self.statb], [junkb, self.statb], accum=ssq)
            self.rstd(rs, ssq, 1.0 / 512)
            self.act(vtok[:, t, :], gv[k], AF.Copy, [gvb[k], self.statb], [vtokb[t]], scale=rs)
        slot, sbf = self.wload(self.win(l, 2056, 512), 512)
        for c in range(4):
            for tg in range(4):
                ps, pb = self.proj_fm(slot, sbf, c * 128, tg)
                self.act(uT[:, c, tg * 512:(tg + 1) * 512], ps[:, :], AF.Gelu_apprx_tanh, [pb], [uTb[c][tg]])
        zslot, zsb = self.wload(self.win(l, 3080, 512), 512)
        it = 0
        for g in range(4):
            for tg in range(4):
                k = it % 2
                it += 1
                ps, pb = self.newps()
                for j in range(4):
                    n = tg * 4 + j
                    self.mm(ps[:, j * 128:(j + 1) * 128], vtok[:, n, g * 128:(g + 1) * 128], wsT[:, g, :], True, True, [vtokb[n], wsTb], [pb])
                sbt = self.prmB_t[:, P_SB + g * 128:P_SB + (g + 1) * 128]
                self.stt("dve", ssb[k].rearrange("p (a b) -> p a b", a=4), ps[:, :].rearrange("p (a b) -> p a b", a=4),
                         self.prm(l, P_GMN + g, 1), sbt.unsqueeze(1).to_broadcast([128, 4, 128]), ALU.mult, ALU.add,
                         [pb, self.prmb, self.prmBb], [ssbb[k]])
                pz, pzb = self.proj_fm(zslot, zsb, g * 128, tg)
                self.act(sz[k], pz[:, :], AF.Silu, [pzb], [szb[k]])
                self.tt("dve", ssb[k], ssb[k], uT[:, g, tg * 512:(tg + 1) * 512], ALU.mult, [ssbb[k], uTb[g][tg]], [ssbb[k]])
                self.tt("dve", self.yT[1][:, g, tg * 512:(tg + 1) * 512], ssb[k], sz[k], ALU.mult, [ssbb[k], szb[k]], [self.yTb[1][g][tg]])
        if self.dbg == 12 and l == 0 and self.branches[-1] == "b":
            for h in range(4):
                self.dump(h, self.yT[1][:, h, :], self.yTb[1][h])

    def branch_c(self, l):
        P = self.P
        P.barrier()
        ar = Arena(self.big[:, 8192:12288 + self.TAIL], 4096 + self.TAIL)
        qT = ar.bf16(4 * T).rearrange("p (c t) -> p c t", c=4)
        qTb = [[Buf() for _ in range(4)] for _ in range(4)]
        kp = ar.bf16(4 * T).rearrange("p (c t) -> p c t", c=4)
        kpb = [[Buf() for _ in range(4)] for _ in range(4)]
        vz = ar.bf16(16 * 2 * 192).rearrange("p (t g d) -> p t g d", t=16, g=2)
        vzb = [Buf() for _ in range(16)]
        es = ar.f32(4)
        esb = Buf()
        pT = [[ar.bf16(512) for kb in range(2)] for _ in range(2)]
        pTb = [[Buf(), Buf()], [Buf(), Buf()]]
        rden = [ar.f32(256) for _ in range(2)]
        rdenb = [Buf(), Buf()]
        sz = [ar.f32(512) for _ in range(2)]
        szb = [Buf(), Buf()]
        self.act(es, self.prm(l, P_SINK, 4), AF.Exp, [self.prmb], [esb])
        self.memset("dve", vz[:, :, :, :], 0.0, vzb)
        slot, sbf = self.wload(self.win(l, 3592, 512), 512)
        for c in range(4):
            for tg in range(4):
                ps, pb = self.proj_fm(slot, sbf, c * 128, tg)
                self.cp("act", qT[:, c, tg * 512:(tg + 1) * 512], ps[:, :], [pb], [qTb[c][tg]])
        i = self.ws_i
        self.ws_i = (i + 1) % len(self.ws)
        slot, sbf = self.ws[i], self.wsb[i]
        self.memset("pool", slot[:, :, :], 0.0, [sbf])
        for (co, src) in ((0, 4104), (192, 4104), (256, 4168), (448, 4168)):
            self.dma("pool", slot[:, :, co:co + 64], self.win(l, src, 64), (), [sbf])
        for j in range(4):
            for tg in range(4):
                ps, pb = self.proj_fm(slot, sbf, j * 128, tg)
                self.cp("act", kp[:, j, tg * 512:(tg + 1) * 512], ps[:, :], [pb], [kpb[j][tg]])
        slot, sbf = self.wload(self.win(l, 4232, 128), 128)
        for t in range(NT):
            ps, pb = self.newps()
            for kc in range(8):
                self.mm(ps[:, 0:128], self.hT[:, kc, t * 128:(t + 1) * 128], slot[:, kc, 0:128], kc == 0, kc == 7, [sbf, self.hTb[t]], [pb])
            self.cp("act", vz[:, t, :, 64:128], ps[:, 0:128].rearrange("p (g d) -> p g d", g=2), [pb], [vzb[t]])
        it = 0
        for n in range(NT):
            tb = slice(n * 128, (n + 1) * 128)
            for g in range(2):
                k = it % 2
                it += 1
                kbs = [1] if n == 0 else [0, 1]
                for kb in kbs:
                    blk = n - 1 + kb
                    sbk = slice(blk * 128, (blk + 1) * 128)
                    ps, pb = self.newps()
                    for par in range(2):
                        self.mm(ps[:, par * 256:(par + 1) * 256].rearrange("p (a b) -> p a b", a=2), kp[:, g * 2 + par, sbk], qT[:, 2 * g:2 * g + 2, tb],
                                True, True, [kpb[g * 2 + par][blk // 4], qTb[2 * g][n // 4], qTb[2 * g + 1][n // 4]], [pb])
                    self.act(pT[k][kb], ps[:, :], AF.Exp, [pb], [pTb[k][kb]], scale=0.125)
                    msk = self.cst(C_TRI) if kb == 1 else self.cst(C_STRICT)
                    self.tt("dve", pT[k][kb].rearrange("p (a b) -> p a b", a=4), pT[k][kb].rearrange("p (a b) -> p a b", a=4),
                            msk.unsqueeze(1).to_broadcast([128, 4, 128]), ALU.mult, [pTb[k][kb], self.cstb], [pTb[k][kb]])
                po, pob = self.newps()
                combos = [(kb, par) for kb in kbs for par in range(2)]
                for ci, (kb, par) in enumerate(combos):
                    blk = n - 1 + kb
                    lhs = vz[:, blk, g, 64:192] if par == 0 else vz[:, blk, g, 0:128]
                    self.mm(po[:, 0:256], lhs, pT[k][kb][:, par * 256:(par + 1) * 256], ci == 0, ci == len(combos) - 1, [vzb[blk], pTb[k][kb]], [pob])
                for ci, (kb, par) in enumerate(combos):
                    lhs = self.onesE if par == 0 else self.onesO
                    self.mm(po[:, 256:512], lhs, pT[k][kb][:, par * 256:(par + 1) * 256], ci == 0, ci == len(combos) - 1, [self.cb16b, pTb[k][kb]], [pob])
                self.tt("dve", rden[k].rearrange("p (a b) -> p a b", a=2), po[:, 256:512].rearrange("p (a b) -> p a b", a=2),
                        es[:, 2 * g:2 * g + 2].unsqueeze(2).to_broadcast([128, 2, 128]), ALU.add, [pob, esb], [rdenb[k]])
                self.act(rden[k], rden[k], AF.Ln, [rdenb[k]], [rdenb[k]])
                self.act(rden[k], rden[k], AF.Exp, [rdenb[k]], [rdenb[k]], scale=-1.0)
                self.tt("dve", self.yT[2][:, 2 * g:2 * g + 2, tb], po[:, 0:256].rearrange("p (a b) -> p a b", a=2), rden[k].rearrange("p (a b) -> p a b", a=2),
                        ALU.mult, [pob, rdenb[k]], [self.yTb[2][2 * g][n // 4], self.yTb[2][2 * g + 1][n // 4]])
        zslot, zsb = self.wload(self.win(l, 4360, 512), 512)
        it = 0
        for c in range(4):
            for tg in range(4):
                k = it % 2
                it += 1
                pz, pzb = self.proj_fm(zslot, zsb, c * 128, tg)
                self.act(sz[k], pz[:, :], AF.Silu, [pzb], [szb[k]])
                self.tt("dve", self.yT[2][:, c, tg * 512:(tg + 1) * 512], self.yT[2][:, c, tg * 512:(tg + 1) * 512], sz[k], ALU.mult,
                        [self.yTb[2][c][tg], szb[k]], [self.yTb[2][c][tg]])
        if self.dbg == 12 and l == 0 and self.branches[-1] == "c":
            for h in range(4):
                self.dump(h, self.yT[2][:, h, :], self.yTb[2][h])

    def branch_m(self, l):
        P = self.P
        P.barrier()
        ar = Arena(self.arena_t, self.ARN)
        qT = ar.bf16(4 * T).rearrange("p (h t) -> p h t", h=4)
        qTb = [[Buf() for _ in range(4)] for _ in range(4)]
        mkT = ar.bf16(4 * MEM).rearrange("p (h m) -> p h m", h=4)
        mkTb = Buf()
        mv = ar.bf16(2 * 512).rearrange("p (c f) -> p c f", c=2)
        mvb = Buf()
        pT = [ar.bf16(1024).rearrange("p (c t) -> p c t", c=2) for _ in range(2)]
        pTb = [Buf(), Buf()]
        rden = [ar.f32(512) for _ in range(2)]
        rdenb = [Buf(), Buf()]
        otmp = [ar.f32(512) for _ in range(2)]
        otmpb = [Buf(), Buf()]
        sz = [ar.f32(512) for _ in range(2)]
        szb = [Buf(), Buf()]
        for t in range(2):
            xi, xb = self.xin[t % 2], self.xinb[t % 2]
            self.dma("sp", xi[:, :], self.mem[t * 128:(t + 1) * 128, :], (), [xb])
            ssq = self.stat[:, 20 + t:21 + t]
            rs = self.stat[:, 52 + t:53 + t]
            self.memset("dve", ssq, 0.0, [self.statb])
            self.act(self.xsc[:, :], xi[:, :], AF.Square, [xb, self.statb], [self.xscb, self.statb], accum=ssq)
            self.rstd(rs, ssq, 1.0 / D)
            self.act(self.xsc[:, :], xi[:, :], AF.Copy, [xb, self.statb], [self.xscb], scale=rs)
            for half in range(2):
                ps, pb = self.newps()
                for j in range(4):
                    kc = half * 4 + j
                    self.tr(ps[:, j * 128:(j + 1) * 128], self.xsc[:, kc * 128:(kc + 1) * 128], self.cst(C_ID), [self.xscb, self.cstb], [pb])
                g = self.prm(l, P_GMEM + half * 4, 4)
                self.tt("dve", self.memT[:, half * 4:half * 4 + 4, t * 128:(t + 1) * 128], ps[:, :].rearrange("p (a b) -> p a b", a=4),
                        g.unsqueeze(2).to_broadcast([128, 4, 128]), ALU.mult, [pb, self.prmb], [self.memTb])
        slot, sbf = self.wload(self.w_mem[l][:, 0:512].rearrange("(kc p) c -> p kc c", p=128), 512)
        for h in range(4):
            ps, pb = self.newps()
            for kc in range(8):
                self.mm(ps[:, 0:MEM], slot[:, kc, h * 128:(h + 1) * 128], self.memT[:, kc, :], kc == 0, kc == 7, [sbf, self.memTb], [pb])
            self.cp("act", mkT[:, h, :], ps[:, 0:MEM], [pb], [mkTb])
        slot, sbf = self.wload(self.w_mem[l][:, 512:1024].rearrange("(kc p) c -> p kc c", p=128), 512)
        for mc in range(2):
            ps, pb = self.newps()
            for kc in range(8):
                self.mm(ps[:, :], self.memT[:, kc, mc * 128:(mc + 1) * 128], slot[:, kc, :], kc == 0, kc == 7, [sbf, self.memTb], [pb])
            self.cp("act", mv[:, mc, :], ps[:, :], [pb], [mvb])
        slot, sbf = self.wload(self.win(l, 4872, 512), 512)
        for h in range(4):
            for tg in range(4):
                ps, pb = self.proj_fm(slot, sbf, h * 128, tg)
                self.cp("act", qT[:, h, tg * 512:(tg + 1) * 512], ps[:, :], [pb], [qTb[h][tg]])
        zslot, zsb = self.wload(self.win(l, 5384, 512), 512)
        it = 0
        for h in range(4):
            for tg in range(4):
                k = it % 2
                it += 1
                pss = []
                for mc in range(2):
                    ps, pb = self.newps()
                    self.mm(ps[:, :], mkT[:, h, mc * 128:(mc + 1) * 128], qT[:, h, tg * 512:(tg + 1) * 512], True, True, [mkTb, qTb[h][tg]], [pb])
                    self.act(pT[k][:, mc, :], ps[:, :], AF.Exp, [pb], [pTb[k]], scale=128 ** -0.5)
                po, pob = self.newps()
                for mc in range(2):
                    self.mm(po[:, :], mv[:, mc, h * 128:(h + 1) * 128], pT[k][:, mc, :], mc == 0, mc == 1, [mvb, pTb[k]], [pob])
                pd, pdb = self.newps()
                for mc in range(2):
                    self.mm(pd[:, :], self.onesb, pT[k][:, mc, :], mc == 0, mc == 1, [self.cb16b, pTb[k]], [pdb])
                self.act(rden[k], pd[:, :], AF.Ln, [pdb], [rdenb[k]])
                self.act(rden[k], rden[k], AF.Exp, [rdenb[k]], [rdenb[k]], scale=-1.0)
                self.tt("dve", otmp[k], po[:, :], rden[k], ALU.mult, [pob, rdenb[k]], [otmpb[k]])
                pz, pzb = self.proj_fm(zslot, zsb, h * 128, tg)
                self.act(sz[k], pz[:, :], AF.Silu, [pzb], [szb[k]])
                self.tt("dve", self.yT[3][:, h, tg * 512:(tg + 1) * 512], otmp[k], sz[k], ALU.mult, [otmpb[k], szb[k]], [self.yTb[3][h][tg]])
        if self.dbg == 12 and l == 0:
            for h in range(4):
                self.dump(h, self.yT[3][:, h, :], self.yTb[3][h])

    def merge_out(self, l):
        P = self.P
        P.barrier()
        ar = Arena(self.arena_t, self.ARN)
        if self.dbg == 28 and l == 0:
            for i in range(4):
                for c in range(4):
                    self.dump(12 + i * 4 + c, self.yT[i][:, c, :], self.yTb[i][c])
        mT = ar.bf16(8 * T).rearrange("p (c t) -> p c t", c=8)
        mTb = [[Buf() for _ in range(4)] for _ in range(8)]
        sg = [ar.f32(512) for _ in range(2)]
        sgb = [Buf(), Buf()]
        acc = [ar.f32(512) for _ in range(2)]
        accb = [Buf(), Buf()]
        tmp = [ar.f32(512) for _ in range(2)]
        tmpb = [Buf(), Buf()]
        it = 0
        ia = 0
        for dc in range(8):
            i = self.ws_i
            self.ws_i = (i + 1) % len(self.ws)
            gslot, gsb = self.ws[i], self.wsb[i]
            for n in range(4):
                c0 = 5896 + n * 1024 + dc * 128
                self.dma("pool", gslot[:, :, n * 128:(n + 1) * 128], self.win(l, c0, 128), (), [gsb])
            i = self.ws_i
            self.ws_i = (i + 1) % len(self.ws)
            uslot, usb = self.ws[i][:, :, :].rearrange("p a (b c) -> p (a b) c", c=128), self.wsb[i]
            for n in range(4):
                self.dma("pool", uslot[:, n * 4:(n + 1) * 4, :], self.w_up[l][n][:, dc * 128:(dc + 1) * 128].rearrange("(kc p) d -> p kc d", p=128), (), [usb])
            for tg in range(4):
                ka = ia % 2
                ia += 1
                for n in range(4):
                    k = it % 2
                    it += 1
                    pg, pgb = self.proj_fm(gslot, gsb, n * 128, tg)
                    self.act(sg[k], pg[:, :], AF.Sigmoid, [pgb], [sgb[k]])
                    pp, ppb = self.newps()
                    for kc in range(4):
                        self.mm(pp[:, :], uslot[:, n * 4 + kc, :], self.yT[n][:, kc, tg * 512:(tg + 1) * 512], kc == 0, kc == 3,
                                [usb, self.yTb[n][kc][tg]], [ppb])
                    if n == 0:
                        self.tt("dve", acc[ka], pp[:, :], sg[k], ALU.mult, [ppb, sgb[k]], [accb[ka]])
                    else:
                        self.tt("dve", tmp[k], pp[:, :], sg[k], ALU.mult, [ppb, sgb[k]], [tmpb[k]])
                        if n < 3:
                            self.tt("dve", acc[ka], acc[ka], tmp[k], ALU.add, [accb[ka], tmpb[k]], [accb[ka]])
                        else:
                            self.tt("dve", mT[:, dc, tg * 512:(tg + 1) * 512], acc[ka], tmp[k], ALU.add, [accb[ka], tmpb[k]], [mTb[dc][tg]])
        if self.dbg and l == 0:
            for dc in range(8):
                self.dump(4 + dc, mT[:, dc, :], mTb[dc])
        P.barrier()
        ar.off = 8 * T // 2
        wo = []
        for hh in range(2):
            slot, sbf = self.wload(self.w_out[l][:, hh * 512:(hh + 1) * 512].rearrange("(kc p) c -> p kc c", p=128), 512)
            wo.append((slot, sbf))
        osb = [ar.f32(1024) for _ in range(2)]
        osbb = [Buf(), Buf()]
        xo = [ar.f32(1024) for _ in range(2)]
        xob = [Buf(), Buf()]
        src = self.x if l == 0 else self.xs
        last = (l == self.nl - 1)
        dst = self.y if last else self.xs
        for t in range(NT):
            k = t % 2
            xi, xb = self.xin[k], self.xinb[k]
            self.dma("sp", xi[:, :], src[t * 128:(t + 1) * 128, :], [self.out_b] if l > 0 else (), [xb])
            pso = []
            for hh in range(2):
                ps, pb = self.newps()
                slot, sbf = wo[hh]
                for kc in range(8):
                    self.mm(ps[:, :], mT[:, kc, t * 128:(t + 1) * 128], slot[:, kc, :], kc == 0, kc == 7, [sbf, mTb[kc][t // 4]], [pb])
                pso.append((ps, pb))
            ssq = self.stat[:, 24 + k:25 + k]
            rs = self.stat[:, 56 + k:57 + k]
            ss2 = self.stat[:, 26 + k:27 + k]
            self.memset("dve", ssq, 0.0, [self.statb])
            self.memset("dve", ss2, 0.0, [self.statb])
            self.act(osb[k][:, 0:512], pso[0][0][:, :], AF.Square, [pso[0][1], self.statb], [osbb[k], self.statb], accum=ssq)
            self.act(osb[k][:, 512:1024], pso[1][0][:, :], AF.Square, [pso[1][1], self.statb], [osbb[k], self.statb], accum=ss2)
            self.tt("dve", ssq, ssq, ss2, ALU.add, [self.statb], [self.statb])
            self.rstd(rs, ssq, 1.0 / D)
            gp = self.prmB_t[:, P_GPOST:P_GPOST + 1024]
            for hh in range(2):
                self.stt("dve", osb[k][:, hh * 512:(hh + 1) * 512], pso[hh][0][:, :], rs, gp[:, hh * 512:(hh + 1) * 512], ALU.mult, ALU.mult,
                         [pso[hh][1], self.statb, self.prmBb], [osbb[k]])
            self.tt("pool" if False else "dve", xo[k], osb[k], xi[:, :], ALU.add, [osbb[k], xb], [xob[k]])
            self.dma("sp", dst[t * 128:(t + 1) * 128, :], xo[k], [xob[k]], [self.out_b], key="outd")
            if not last:
                self.norm_T_sb(xo[k], xob[k], l + 1, t)

    def norm_T_sb(self, xi, xb, l, t):
        ssq = self.stat[:, t:t + 1]
        rs = self.stat[:, 32 + t:33 + t]
        self.memset("dve", ssq, 0.0, [self.statb])
        self.act(self.xsc[:, :], xi, AF.Square, [xb, self.statb], [self.xscb, self.statb], accum=ssq)
        self.rstd(rs, ssq, 1.0 / D)
        self.act(self.xsc[:, :], xi, AF.Copy, [xb, self.statb], [self.xscb], scale=rs)
        for half in range(2):
            ps, pb = self.newps()
            for j in range(4):
                kc = half * 4 + j
                self.tr(ps[:, j * 128:(j + 1) * 128], self.xsc[:, kc * 128:(kc + 1) * 128], self.cst(C_ID), [self.xscb, self.cstb], [pb])
            g = self.prm(l, P_GPRE + half * 4, 4)
            self.tt("dve", self.hT[:, half * 4:half * 4 + 4, t * 128:(t + 1) * 128], ps[:, :].rearrange("p (a b) -> p a b", a=4),
                    g.unsqueeze(2).to_broadcast([128, 4, 128]), ALU.mult, [pb, self.prmb], [self.hTb[t]])


def _consts():
    i = np.arange(128)[:, None]
    j = np.arange(128)[None, :]
    blocks = [(i == j), (i <= j), (i > j), np.broadcast_to(i == 127, (128, 128)), np.ones((128, 128), bool)]
    offs = []
    for k in range(1, 8):
        offs.append((i > j) & ((i >> k) == (j >> k)) & ((i >> (k - 1)) != (j >> (k - 1))))
    blocks += offs
    blocks.append(offs[0].T)
    return np.concatenate([b.astype(np.float32) for b in blocks], axis=1)


def _params(inp):
    prm = np.zeros((128, NL * PL), np.float32)
    for l in range(NL):
        o = l * PL
        prm[:, o + P_GPRE:o + P_GPRE + 8] = inp["norm_pre"][l].reshape(8, 128).T
        prm[:, o + P_GMEM:o + P_GMEM + 8] = inp["norm_mem"][l].reshape(8, 128).T
        prm[:, o + P_CW:o + P_CW + 48] = inp["conv_w"][l].reshape(4, 12, 128).transpose(2, 1, 0).reshape(128, 48)
        prm[:, o + P_DN] = inp["dn_norm"][l]
        prm[:, o + P_GMN:o + P_GMN + 4] = inp["gm_norm"][l].reshape(4, 128).T
        sk = inp["sinks"][l].reshape(4, 2)
        prm[0:64, o + P_SINK:o + P_SINK + 4] = np.broadcast_to(sk[:, 0], (64, 4))
        prm[64:128, o + P_SINK:o + P_SINK + 4] = np.broadcast_to(sk[:, 1], (64, 4))
        prm[:, o + P_ALOG:o + P_ALOG + 4] = np.broadcast_to(inp["a_log"][l], (128, 4))
        prm[:, o + P_DTB:o + P_DTB + 4] = np.broadcast_to(inp["dt_bias"][l], (128, 4))
    prmb = np.zeros((NL, 128, PLB), np.float32)
    for l in range(NL):
        prmb[l, :, P_GPOST:P_GPOST + 1024] = np.broadcast_to(inp["norm_post"][l], (128, 1024))
        prmb[l, :, P_SB:P_SB + 512] = np.broadcast_to(inp["spatial_b"][l].reshape(512), (128, 512))
    return prm, prmb


_NC_CACHE = {}


def make_in_maps(inp, cores):
    f = lambda a: np.ascontiguousarray(np.asarray(a, dtype=np.float32))
    inp = {k: f(v) for k, v in inp.items()}
    shared = {
        "w_in": inp["w_in"], "w_mem_kv": inp["w_mem_kv"], "w_up": inp["w_up"], "w_out": inp["w_out"],
        "prm": _params(inp)[0], "prmb": _params(inp)[1], "cst": _consts(),
        "spw": np.ascontiguousarray(inp["spatial_w"].transpose(0, 1, 3, 2)),
    }
    maps = []
    for b in cores:
        m = dict(shared)
        m["x"] = inp["x"][b]
        m["mem"] = inp["mem"][b]
        maps.append(m)
    return maps


def kernel(**inputs):
    key = "full"
    if key not in _NC_CACHE:
        _NC_CACHE[key] = K().build()
    nc = _NC_CACHE[key]
    maps = make_in_maps(inputs, list(range(8)))
    res = run_bass_kernel_spmd(nc, maps, core_ids=list(range(8)))
    return np.stack([np.asarray(r["y"]) for r in res.results], axis=0).astype(np.float32)
```
